# Optimizing a Trainium2 kernel written in Bass

```python
import jax, jax.numpy as jnp
from jax import lax
import numpy as np

D_MODEL = 1024
BATCH = 16
SEQ = 2048
DEPTH = 2

N_META = 16
POOL_WINDOWS = (2, 4, 8, 16)
POOL_GROUP = 128
POOL_WIDTH = POOL_GROUP * len(POOL_WINDOWS)
N_HEADS = 16
QK_NOPE = 64
QK_ROPE = 32
V_DIM = 64
Q_RANK = 256
KV_RANK = 128
QK_DIM = QK_NOPE + QK_ROPE
ATT_WIDTH = N_HEADS * V_DIM
SM_SCALE = QK_DIM ** -0.5
ROPE_THETA = 10000.0
Q_BLOCK = 128
D_FF = -(-8 * D_MODEL // (3 * 256)) * 256
NORM_EPS = 1e-6
MASK_VALUE = -1e30
IN_SIZES = (POOL_WIDTH, Q_RANK, KV_RANK, QK_ROPE, D_MODEL, D_MODEL)
D_IN = POOL_WIDTH + Q_RANK + KV_RANK + QK_ROPE + 2 * D_MODEL
IN_OFFSETS = (POOL_WIDTH,
              POOL_WIDTH + Q_RANK,
              POOL_WIDTH + Q_RANK + KV_RANK,
              POOL_WIDTH + Q_RANK + KV_RANK + QK_ROPE,
              POOL_WIDTH + Q_RANK + KV_RANK + QK_ROPE + D_MODEL)

kernel_name = "hybrid_pool_mla_gated_block"


def rmsnorm(x, g):
    xf = x.astype(jnp.float32)
    y = xf * lax.rsqrt(jnp.mean(xf * xf, axis=-1, keepdims=True) + NORM_EPS)
    return (y * g.astype(jnp.float32)).astype(x.dtype)


def rope_tables(length):
    inv = 1.0 / (ROPE_THETA ** (jnp.arange(0, QK_ROPE, 2, dtype=jnp.float32) / QK_ROPE))
    ang = jnp.arange(length, dtype=jnp.float32)[:, None] * inv[None, :]
    return jnp.cos(ang), jnp.sin(ang)


def apply_rope(x, cos, sin):
    xf = x.astype(jnp.float32)
    x1, x2 = jnp.split(xf, 2, axis=-1)
    out = jnp.concatenate([x1 * cos - x2 * sin, x1 * sin + x2 * cos], axis=-1)
    return out.astype(x.dtype)


def pool_mixer(u, pool_w, pool_scale):
    B, L, _ = u.shape
    cs = jnp.cumsum(u.astype(jnp.float32), axis=1)
    cs0 = jnp.concatenate([jnp.zeros((B, 1, POOL_WIDTH), jnp.float32), cs], axis=1)
    t = jnp.arange(L, dtype=jnp.float32)[:, None]
    groups = []
    for g, w in enumerate(POOL_WINDOWS):
        c = cs0[:, :, g * POOL_GROUP:(g + 1) * POOL_GROUP]
        prev = jnp.pad(c[:, :L + 1 - w], ((0, 0), (w, 0), (0, 0)))
        wsum = (c - prev)[:, 1:]
        count = jnp.minimum(t + 1.0, float(w))
        ug = u[:, :, g * POOL_GROUP:(g + 1) * POOL_GROUP].astype(jnp.float32)
        groups.append(wsum / count - ug)
    y = jnp.stack(groups, axis=2).astype(u.dtype)
    y = jnp.einsum('blgc,gcd->blgd', y, pool_w).reshape(B, L, POOL_WIDTH)
    return y * pool_scale


def mla_attention(q_nope, q_rope, k_nope, k_rope, v):
    L = q_nope.shape[1]
    outs = []
    for start in range(0, L, Q_BLOCK):
        end = min(start + Q_BLOCK, L)
        s = (jnp.einsum('bqhd,bkhd->bhqk', q_nope[:, start:end], k_nope[:, :end])
             + jnp.einsum('bqhr,bkr->bhqk', q_rope[:, start:end], k_rope[:, :end]))
        s = s.astype(jnp.float32) * SM_SCALE
        mask = jnp.arange(end)[None, :] <= jnp.arange(start, end)[:, None]
        s = jnp.where(mask[None, None], s, MASK_VALUE)
        p = jax.nn.softmax(s, axis=-1).astype(v.dtype)
        outs.append(jnp.einsum('bhqk,bkhd->bqhd', p, v[:, :end]))
    return jnp.concatenate(outs, axis=1)


def hybrid_layer(h, cos, sin, g_mix, w_in, pool_w, pool_scale, q_norm_g, kv_norm_g, w_uq, w_ukv,
                 w_pa, w_pb, w_o, g_ffn, w_gate, w_up, w_down):
    B, L, _ = h.shape
    hn = rmsnorm(h, g_mix)
    z = hn @ w_in
    u, c_q, c_kv, k_rope, gate_a, gate_b = jnp.split(z, IN_OFFSETS, axis=-1)
    a = pool_mixer(u, pool_w, pool_scale)
    q = (rmsnorm(c_q, q_norm_g) @ w_uq).reshape(B, L, N_HEADS, QK_DIM)
    q_nope, q_rope = q[..., :QK_NOPE], q[..., QK_NOPE:]
    q_rope = apply_rope(q_rope, cos[:, None, :], sin[:, None, :])
    kv = (rmsnorm(c_kv, kv_norm_g) @ w_ukv).reshape(B, L, N_HEADS, QK_NOPE + V_DIM)
    k_nope, v = kv[..., :QK_NOPE], kv[..., QK_NOPE:]
    k_rope = apply_rope(k_rope, cos, sin)
    b = mla_attention(q_nope, q_rope, k_nope, k_rope, v).reshape(B, L, ATT_WIDTH)
    merged = jax.nn.sigmoid(gate_a) * (a @ w_pa) + jax.nn.sigmoid(gate_b) * (b @ w_pb)
    h = h + merged @ w_o
    hn = rmsnorm(h, g_ffn)
    h = h + (jax.nn.silu(hn @ w_gate) * (hn @ w_up)) @ w_down
    return h


def setup_inputs(seed: int = 0) -> dict:
    key = jax.random.key(seed)
    ks = jax.random.split(key, 20)

    def w(k, shape, fan_in):
        return jax.random.normal(k, shape, jnp.float32) * (fan_in ** -0.5)

    def gain(k, shape):
        return 1.0 + 0.05 * jax.random.normal(k, shape, jnp.float32)

    return {
        "x": jax.random.normal(ks[0], (BATCH, SEQ, D_MODEL), jnp.float32),
        "meta_tokens": jax.random.normal(ks[1], (N_META, D_MODEL), jnp.float32),
        "norm_mix_g": gain(ks[2], (DEPTH, D_MODEL)),
        "w_in": w(ks[3], (DEPTH, D_MODEL, D_IN), D_MODEL),
        "pool_w": w(ks[4], (DEPTH, len(POOL_WINDOWS), POOL_GROUP, POOL_GROUP), POOL_GROUP),
        "pool_scale": gain(ks[5], (DEPTH, POOL_WIDTH)),
        "q_norm_g": gain(ks[6], (DEPTH, Q_RANK)),
        "kv_norm_g": gain(ks[7], (DEPTH, KV_RANK)),
        "w_uq": w(ks[8], (DEPTH, Q_RANK, N_HEADS * QK_DIM), Q_RANK),
        "w_ukv": w(ks[9], (DEPTH, KV_RANK, N_HEADS * (QK_NOPE + V_DIM)), KV_RANK),
        "w_pa": w(ks[10], (DEPTH, POOL_WIDTH, D_MODEL), POOL_WIDTH),
        "w_pb": w(ks[11], (DEPTH, ATT_WIDTH, D_MODEL), ATT_WIDTH),
        "w_o": w(ks[12], (DEPTH, D_MODEL, D_MODEL), D_MODEL),
        "norm_ffn_g": gain(ks[13], (DEPTH, D_MODEL)),
        "w_gate": w(ks[14], (DEPTH, D_MODEL, D_FF), D_MODEL),
        "w_up": w(ks[15], (DEPTH, D_MODEL, D_FF), D_MODEL),
        "w_down": w(ks[16], (DEPTH, D_FF, D_MODEL), D_FF),
        "final_norm_g": gain(ks[17], (D_MODEL,)),
    }


def reference(x, meta_tokens, norm_mix_g, w_in, pool_w, pool_scale, q_norm_g, kv_norm_g, w_uq, w_ukv,
              w_pa, w_pb, w_o, norm_ffn_g, w_gate, w_up, w_down, final_norm_g):
    B = x.shape[0]
    meta = jnp.broadcast_to(meta_tokens.astype(x.dtype)[None], (B, N_META, D_MODEL))
    h = jnp.concatenate([meta, x], axis=1)
    cos, sin = rope_tables(h.shape[1])
    for i in range(DEPTH):
        h = hybrid_layer(h, cos, sin, norm_mix_g[i], w_in[i], pool_w[i], pool_scale[i], q_norm_g[i],
                         kv_norm_g[i], w_uq[i], w_ukv[i], w_pa[i], w_pb[i], w_o[i], norm_ffn_g[i],
                         w_gate[i], w_up[i], w_down[i])
    return rmsnorm(h, final_norm_g)[:, N_META:]
```

```python
import numpy as np
from contextlib import ExitStack
import concourse.bass as bass
import concourse.mybir as mybir
from concourse.bass_utils import run_bass_kernel_spmd

F32 = mybir.dt.float32
BF16 = mybir.dt.bfloat16
U8 = mybir.dt.uint8
AF = mybir.ActivationFunctionType
ALU = mybir.AluOpType

ENGS = ("pe", "act", "dve", "pool", "sp")

D = 1024
KC = 8
NM = 16
SEQ = 2048
L = NM + SEQ
LP = 17 * 128
NS = 2
NCORES = 8
DEPTH = 2
NH = 16
DFF = 2816
FC = DFF // 128
EPS = 1e-6
SM_SCALE = 96 ** -0.5
POOL_W = (2, 4, 8, 16)
NT_LAYER = 33
TW = 4096

T512 = [(0, 512), (512, 512), (1024, 512), (1536, 512), (2048, 16)]
KT = [(128 * i, 128) for i in range(17)]
QB = [(0, 16), (16, 512), (528, 512), (1040, 512), (1552, 512)]
BLK = 344
NSPAN = 3
SPAN = L // NSPAN
NBLK = SPAN // BLK

SM_GMIX, SM_GFFN, SM_PSC, SM_QG, SM_KVG, SM_GFIN, SM_N = 0, 8, 16, 20, 22, 23, 31


class Buf:
    __slots__ = ("name", "w", "r")

    def __init__(self, name):
        self.name = name
        self.w = []
        self.r = []


class Prog:
    def __init__(self, nc):
        self.nc = nc
        self.ops = {e: [] for e in ENGS}
        self.cnt = {e: 0 for e in ENGS}
        self.waited = {e: {} for e in ENGS}
        self.dma_sems = []

    def _need(self, eng, toks):
        w = self.waited[eng]
        best = {}
        for t in toks:
            if t is None:
                continue
            key = (t[0], t[1])
            if t[0] == "e" and t[1] == "pe" and eng == "pe":
                continue
            if w.get(key, 0) >= t[2]:
                continue
            if best.get(key, 0) < t[2]:
                best[key] = t[2]
        out = []
        for key, v in best.items():
            w[key] = v
            out.append((key[0], key[1], v))
        return out

    def _collect(self, reads, writes, extra):
        toks = list(extra)
        for b in reads:
            toks += b.w
        for b in writes:
            toks += b.w
            toks += b.r
        return toks

    def op(self, eng, fn, reads=(), writes=(), inc=True, extra=()):
        waits = self._need(eng, self._collect(reads, writes, extra))
        if inc:
            self.cnt[eng] += 1
            tok = ("e", eng, self.cnt[eng])
        else:
            tok = ("e", eng, self.cnt[eng] + 1)
        self.ops[eng].append((waits, fn, inc, None))
        for b in reads:
            b.r.append(tok)
        for b in writes:
            b.w = [t for t in b.w if (t[0], t[1]) != (tok[0], tok[1])] + [tok]
            b.r = []
        return tok

    def new_dma_sem(self):
        self.dma_sems.append(0)
        return len(self.dma_sems) - 1

    def dma(self, eng, fn, sem, reads=(), writes=(), extra=()):
        waits = self._need(eng, self._collect(reads, writes, extra))
        self.dma_sems[sem] += 16
        tok = ("d", sem, self.dma_sems[sem])
        self.ops[eng].append((waits, fn, False, sem))
        for b in reads:
            b.r.append(tok)
        for b in writes:
            b.w = [t for t in b.w if (t[0], t[1]) != (tok[0], tok[1])] + [tok]
            b.r = []
        return tok

    def wait_all(self, eng, toks):
        waits = self._need(eng, toks)
        if waits:
            self.ops[eng].append((waits, None, False, None))

    def now(self, engs=("pe", "act", "dve", "pool")):
        return [("e", e, self.cnt[e]) for e in engs if self.cnt[e] > 0]

    def barrier(self, engs=("pe", "act", "dve", "pool"), extra=()):
        toks = self.now(engs) + list(extra)
        for e in engs:
            self.wait_all(e, toks)

    def emit(self):
        nc = self.nc
        with ExitStack() as st:
            esem = {e: st.enter_context(nc.semaphore("s_" + e)) for e in ENGS}
            dsem = [st.enter_context(nc.semaphore("d_%d" % i)) for i in range(len(self.dma_sems))]
            block = st.enter_context(nc.Block())

            def run(e, engine):
                for waits, fn, inc, dsi in self.ops[e]:
                    for (k, a, v) in waits:
                        engine.wait_ge(esem[a] if k == "e" else dsem[a], v)
                    if fn is None:
                        continue
                    ins = fn(engine)
                    if dsi is not None:
                        ins.then_inc(dsem[dsi], 16)
                    elif inc:
                        ins.then_inc(esem[e], 1)

            @block.tensor
            def _(eng):
                run("pe", eng)

            @block.scalar
            def _(eng):
                run("act", eng)

            @block.vector
            def _(eng):
                run("dve", eng)

            @block.gpsimd
            def _(eng):
                run("pool", eng)

            @block.sync
            def _(eng):
                run("sp", eng)


class _Stop(Exception):
    pass


_DBG = {}


def build_program(stop_at=None):
    nc = bass.Bass("TRN2", target_bir_lowering=False)
    dram = lambda n, s, d, k: nc.dram_tensor(n, s, d, kind=k).ap()
    xT = dram("xT", [NS, D, SEQ], F32, "ExternalInput")
    metaT = dram("metaT", [D, NM], F32, "ExternalInput")
    wst = dram("wst", [DEPTH * NT_LAYER, 128, TW], F32, "ExternalInput")
    smalls_d = dram("smalls", [128, DEPTH * SM_N], F32, "ExternalInput")
    poolw_d = dram("poolw", [DEPTH, 4, 128, 128], F32, "ExternalInput")
    rope_d = dram("rope", [128, L], F32, "ExternalInput")
    tri_d = dram("tri", [128, 256], F32, "ExternalInput")
    invc_d = dram("invc", [128, 64], F32, "ExternalInput")
    outT = dram("outT", [NS, D, SEQ], F32, "ExternalOutput")

    with ExitStack() as st:
        sbt = lambda n, s, d: st.enter_context(nc.sbuf_tensor("sb_" + n, s, d))
        hT_t = sbt("hT", [128, KC * L], F32)
        hT = hT_t[:, :].rearrange("p (c t) -> p c t", c=KC)
        bT_t = sbt("bT", [128, KC * L], BF16)
        bT = bT_t[:, :].rearrange("p (c t) -> p c t", c=KC)
        rope = sbt("rope", [128, L], F32)
        WB = [sbt("wb%d" % i, [128, TW], BF16) for i in range(3)]
        ones = sbt("ones", [128, 128], BF16)
        tri = sbt("tri", [128, 256], BF16)
        smalls = sbt("smalls", [128, DEPTH * SM_N], F32)
        invc = sbt("invc", [128, 64], F32)
        poolw_t = sbt("poolw", [128, 4 * 128], BF16)
        poolw = poolw_t[:, :].rearrange("p (g d) -> p g d", g=4)
        halo = sbt("halo", [128, 64], F32)
        ARENA_BYTES = int(nc.sbuf_bytes_remaining) - 2048
        arena = sbt("arena", [128, ARENA_BYTES], U8)
        PS = [st.enter_context(nc.psum_tensor("ps%d" % i, [128, 512], F32)) for i in range(8)]
        PSB = [Buf("ps%d" % i) for i in range(8)]

        P = Prog(nc)

        class Arena:
            def __init__(self):
                self.off = 0

            def alloc(self, shape_free, dt):
                esz = 4 if dt == F32 else 2
                n = int(np.prod(shape_free))
                nb = n * esz
                self.off = (self.off + 31) // 32 * 32
                o = self.off
                self.off += nb
                assert self.off <= ARENA_BYTES, ("arena overflow", self.off, ARENA_BYTES)
                v = arena[:, o:o + nb].bitcast(dt)
                _DBG.setdefault("allocs", []).append((o, nb, "f32" if dt == F32 else "bf16"))
                return v

        wsem = [P.new_dma_sem() for _ in range(3)]
        WBB = [Buf("wb%d" % i) for i in range(3)]
        s_const = P.new_dma_sem()
        s_h = [P.new_dma_sem() for _ in range(len(T512))]
        s_out = P.new_dma_sem()
        s_pw = P.new_dma_sem()
        wctr = [0]

        def MM1(out, lhsT, rhs, start, stop, reads=(), writes=(), inc=True):
            return P.op("pe", lambda e: e.matmul(out, lhsT, rhs, start=start, stop=stop), reads=reads, writes=writes, inc=inc)

        def mm_group(out, pairs, reads, psb):
            n = len(pairs)
            for i, (lhsT, rhs) in enumerate(pairs):
                MM1(out, lhsT, rhs, i == 0, i == n - 1, reads=reads if i == 0 else (), writes=[psb] if i == 0 else (),
                    inc=(i == n - 1))

        def ACT(out, in_, func, reads, writes, **kw):
            return P.op("act", lambda e: e.activation(out=out, in_=in_, func=func, **kw), reads=reads, writes=writes)

        def TT(out, in0, in1, op, reads, writes, extra=()):
            return P.op("dve", lambda e: e.tensor_tensor(out=out, in0=in0, in1=in1, op=op), reads=reads, writes=writes,
                        extra=extra)

        def STT(out, in0, scalar, in1, op0, op1, reads, writes):
            return P.op("dve", lambda e: e.scalar_tensor_tensor(out=out, in0=in0, scalar=scalar, in1=in1, op0=op0, op1=op1),
                        reads=reads, writes=writes)

        def TS(out, in0, scalar1, op0, reads, writes):
            return P.op("dve", lambda e: e.tensor_scalar(out=out, in0=in0, scalar1=scalar1, scalar2=None, op0=op0),
                        reads=reads, writes=writes)

        def RPOW(out, in_, pw, reads, writes, wbuf, scale=1.0, bias=0.0):
            ACT(out, in_, AF.Ln, reads=reads, writes=[wbuf], scale=scale, bias=bias)
            return ACT(out, out, AF.Exp, reads=[wbuf], writes=writes, scale=pw)

        def PCPY(out, in_, reads, writes):
            return P.op("pool", lambda e: e.tensor_copy(out=out, in_=in_), reads=reads, writes=writes)

        def PTT(out, in0, in1, op, reads, writes):
            return P.op("pool", lambda e: e.tensor_tensor(out=out, in0=in0, in1=in1, op=op), reads=reads, writes=writes)

        def CPY(out, in_, reads, writes):
            return P.op("dve", lambda e: e.tensor_copy(out=out, in_=in_), reads=reads, writes=writes)

        def MSET(ap, val, writes):
            return P.op("dve", lambda e: e.memset(ap, val), writes=writes)

        def DMA(eng, out, in_, sem, reads=(), writes=(), extra=()):
            return P.dma(eng, lambda e: e.dma_start(out=out, in_=in_), sem, reads=reads, writes=writes, extra=extra)

        def wtile(idx, used):
            i = wctr[0] % 3
            wctr[0] += 1
            DMA("pool", WB[i][:, :used], wst[idx, :, :used], wsem[i], writes=[WBB[i]])
            return WB[i], WBB[i]

        psring = {"i": 0, "banks": list(range(8))}

        def ps_next():
            b = psring["banks"][psring["i"] % len(psring["banks"])]
            psring["i"] += 1
            return PS[b], PSB[b]

        def sm(l, col):
            return smalls[:, l * SM_N + col: l * SM_N + col + 1]

        cb = {k: Buf(k) for k in ["rope", "smalls", "invc", "tri", "ones", "poolw", "halo"]}
        DMA("sp", rope[:], rope_d, s_const, writes=[cb["rope"]])
        DMA("sp", smalls[:], smalls_d, s_const, writes=[cb["smalls"]])
        DMA("sp", invc[:], invc_d, s_const, writes=[cb["invc"]])
        tok_const = ("d", s_const, P.dma_sems[s_const])
        for k in ["rope", "smalls", "invc"]:
            cb[k].w = [tok_const]
        s_tri = P.new_dma_sem()
        DMA("pool", tri[:], tri_d, s_tri, writes=[cb["tri"]])
        MSET(ones[:], 1.0, [cb["ones"]])

        A12 = Arena()
        cqnT = A12.alloc([2 * L], BF16).rearrange("p (c t) -> p c t", c=2)
        ckvnT = A12.alloc([LP], BF16)
        KhT = [A12.alloc([LP], BF16) for _ in range(2)]
        QhT = [A12.alloc([L], BF16) for _ in range(2)]
        Vh = [A12.alloc([17 * 128], BF16).rearrange("p (k d) -> p k d", k=17) for _ in range(2)]
        PT = [A12.alloc([512], BF16) for _ in range(4)]
        hnbs = [A12.alloc([KC * 512], BF16).rearrange("p (c t) -> p c t", c=KC) for _ in range(2)]
        sqq = A12.alloc([3 * 512], BF16).rearrange("p (c t) -> p c t", c=3)
        sqm = [A12.alloc([512], BF16) for _ in range(2)]
        srt = None
        rstd1 = A12.alloc([512], F32)
        srq = srt
        rq = A12.alloc([512], F32)
        srk = srt
        rk = A12.alloc([512], F32)
        rt1 = A12.alloc([512], F32)
        rcp = [A12.alloc([512], F32) for _ in range(2)]

        A3 = Arena()
        W = 16 + SPAN
        hn = A3.alloc([KC * SPAN], BF16).rearrange("p (c t) -> p c t", c=KC)
        actb = A3.alloc([FC * SPAN], BF16)
        act_off0 = A3.off - FC * SPAN * 2
        actT = actb.rearrange("p (c t) -> p c t", c=FC)
        srt3 = None
        rstd3 = A3.alloc([BLK], F32)
        sq3 = [A3.alloc([BLK], BF16) for _ in range(2)]
        sil = [A3.alloc([BLK], F32) for _ in range(2)]
        sg = [A3.alloc([BLK], F32) for _ in range(4)]
        mt = [A3.alloc([BLK], F32) for _ in range(4)]
        ost = A3.alloc([KC * BLK], F32).rearrange("p (c t) -> p c t", c=KC)
        tmp16 = A3.alloc([16], F32)
        Wa = A3.alloc([4 * 128], BF16).rearrange("p (g d) -> p g d", g=4)
        Wn = A3.alloc([4 * 128], BF16).rearrange("p (g d) -> p g d", g=4)
        halo3 = A3.alloc([4 * 16], BF16).rearrange("p (g t) -> p g t", g=4)
        A3b = Arena()
        A3b.off = act_off0
        merged = A3b.alloc([KC * SPAN], BF16).rearrange("p (c t) -> p c t", c=KC)
        aT = A3b.alloc([4 * SPAN], BF16).rearrange("p (c t) -> p c t", c=4)
        Ubf = A3b.alloc([4 * W], BF16).rearrange("p (g t) -> p g t", g=4)
        assert A3b.off <= act_off0 + FC * SPAN * 2, "alias region overflow"

        hTB = [Buf("hT%d" % i) for i in range(len(T512))]
        ostB = Buf("ost")

        def rms_block(srcs, n, gcols, dsts, rd, wr, sq_bufs, sqB, srt_ap, rstd_ap, srtB, rstdB, inv_n=1.0 / D):
            st8 = rms_part1(srcs, n, rd, sq_bufs, sqB)
            rms_part2(st8, srcs, n, gcols, dsts, rd, wr, rstd_ap, rstdB, inv_n)

        def rms_part1(srcs, n, rd, sq_bufs, sqB):
            nch = len(srcs)
            psA, psAB = ps_next()
            for c in range(nch):
                j = c % 2
                ACT(sq_bufs[j][:, :n], srcs[c], AF.Square, reads=rd, writes=[sqB[j]])
                MM1(psA[:, :n], ones[:, :], sq_bufs[j][:, :n], c == 0, c == nch - 1,
                    reads=[sqB[j], cb["ones"]], writes=[psAB], inc=True)
            return psA, psAB

        def rms_part2(st8, srcs, n, gcols, dsts, rd, wr, rstd_ap, rstdB, inv_n=1.0 / D):
            psA, psAB = st8
            RPOW(rstd_ap[:, :n], psA[:, :n], -0.5, reads=[psAB], writes=[rstdB], wbuf=rstdB, scale=inv_n, bias=EPS)
            for c in range(len(srcs)):
                STT(dsts[c], srcs[c], gcols[c], rstd_ap[:, :n], ALU.mult, ALU.mult,
                    reads=list(rd) + [rstdB, cb["smalls"]], writes=wr)

        def stop_check(s, l, ph):
            if stop_at == (s, l, ph):
                raise _Stop()

        preloaded = set()

        def load_x(s, bis):
            prev = P.now()
            xs = xT[s].rearrange("(c p) t -> p c t", p=128)
            ms = metaT.rearrange("(c p) t -> p c t", p=128)
            for bi in bis:
                t0, n = T512[bi]
                lo = max(t0, NM)
                DMA("sp", hT[:, :, lo:t0 + n], xs[:, :, lo - NM:t0 + n - NM], s_h[bi], writes=[hTB[bi]], extra=prev)
                if t0 == 0:
                    DMA("sp", hT[:, :, 0:NM], ms, s_h[bi], writes=[hTB[bi]], extra=prev)
                preloaded.add((s, bi))

        def main_body():
          for s in range(NS):
            load_x(s, [bi for bi in range(len(T512)) if (s, bi) not in preloaded])

            for l in range(DEPTH):
                base = l * NT_LAYER
                DMA("pool", poolw, poolw_d[l].rearrange("g c d -> c g d"), s_pw, writes=[cb["poolw"]])

                psring["banks"] = list(range(8))
                wlat_t, wlatB = wtile(base + 0, KC * 512)
                wlat = wlat_t[:, :].rearrange("p (k c) -> p k c", k=KC)
                B1 = {k: Buf(k) for k in ["hnb0", "hnb1", "sqq", "sqm0", "sqm1", "srt", "rstd", "rq", "rk", "rt1", "rt2",
                                          "cqn", "ckvn", "K0r", "K0r2", "K1r"]}
                B1["srq"] = B1["srt"]
                B1["srk"] = B1["srt"]
                KhB = [Buf("Kh0"), Buf("Kh1")]
                QhB = [Buf("Qh0"), Buf("Qh1")]
                VhB = [Buf("Vh0"), Buf("Vh1")]
                for i in range(2):
                    MSET(Vh[i][:, :, 64:128], 1.0, [VhB[i]])
                    MSET(Vh[i][:, 16, 64:128], 0.0, [VhB[i]])
                    MSET(Vh[i][0:16, 16, 64:128], 1.0, [VhB[i]])
                    MSET(KhT[i][:, L:LP], 0.0, [B1["K%dr" % i]])
                MSET(ckvnT[:, L:LP], 0.0, [B1["ckvn"]])
                def p1_front(bi):
                    t0, n = T512[bi]
                    hnb, hnbB = hnbs[bi % 2], B1["hnb%d" % (bi % 2)]
                    rms_block([hT[:, c, t0:t0 + n] for c in range(KC)], n, [sm(l, SM_GMIX + c) for c in range(KC)],
                              [hnb[:, c, :n] for c in range(KC)], [hTB[bi]], [hnbB],
                              sqm, [B1["sqm0"], B1["sqm1"]], srt, rstd1, B1["srt"], B1["rstd"])

                def p1_z(bi):
                    t0, n = T512[bi]
                    hnb, hnbB = hnbs[bi % 2], B1["hnb%d" % (bi % 2)]
                    zs = []
                    for (c0, M) in [(0, 128), (128, 128), (256, 128), (384, 128)]:
                        ps, psb = ps_next()
                        mm_group(ps[:M, :n], [(wlat[:, k, c0:c0 + M], hnb[:, k, :n]) for k in range(KC)],
                                 [wlatB, hnbB], psb)
                        zs.append((ps, psb))
                    return zs

                def p1_back(bi, zs):
                    t0, n = T512[bi]
                    (pq0, pq0B), (pq1, pq1B), (pkv, pkvB), (pkr, pkrB) = zs
                    for j, (pz, pzB) in enumerate([(pq0, pq0B), (pq1, pq1B), (pkv, pkvB)]):
                        ACT(sqq[:, j, :n], pz[:, :n], AF.Square, reads=[pzB], writes=[B1["sqq"]])
                    psQ, psQB = ps_next()
                    mm_group(psQ[:, :n], [(ones[:, :], sqq[:, 0, :n]), (ones[:, :], sqq[:, 1, :n])], [B1["sqq"], cb["ones"]], psQB)
                    psK, psKB = ps_next()
                    mm_group(psK[:, :n], [(ones[:, :], sqq[:, 2, :n])], [B1["sqq"], cb["ones"]], psKB)
                    RPOW(rq[:, :n], psQ[:, :n], -0.5, reads=[psQB], writes=[B1["rq"]], wbuf=B1["rq"], scale=1.0 / 256, bias=EPS)
                    RPOW(rk[:, :n], psK[:, :n], -0.5, reads=[psKB], writes=[B1["rk"]], wbuf=B1["rk"], scale=1.0 / 128, bias=EPS)
                    for j, (pz, pzB) in enumerate([(pq0, pq0B), (pq1, pq1B)]):
                        STT(cqnT[:, j, t0:t0 + n], pz[:, :n], sm(l, SM_QG + j), rq[:, :n], ALU.mult, ALU.mult,
                            reads=[pzB, B1["rq"], cb["smalls"]], writes=[B1["cqn"]])
                    STT(ckvnT[:, t0:t0 + n], pkv[:, :n], sm(l, SM_KVG), rk[:, :n], ALU.mult, ALU.mult,
                        reads=[pkvB, B1["rk"], cb["smalls"]], writes=[B1["ckvn"]])
                    TT(rt1[0:32, :n], pkr[0:32, :n], rope[0:32, t0:t0 + n], ALU.mult, reads=[pkrB, cb["rope"]], writes=[B1["rt1"]])
                    TT(pkr[32:64, :n], pkr[32:64, :n], rope[32:64, t0:t0 + n], ALU.mult, reads=[pkrB, cb["rope"]], writes=[pkrB])
                    TT(KhT[0][64:96, t0:t0 + n], pkr[32:64, :n], rt1[0:32, :n], ALU.add,
                       reads=[B1["rt1"], pkrB], writes=[B1["K0r"]])
                    PCPY(KhT[0][96:128, t0:t0 + n], KhT[0][64:96, t0:t0 + n], reads=[B1["K0r"]], writes=[B1["K0r2"]])
                    PCPY(KhT[1][64:96, t0:t0 + n], KhT[0][64:96, t0:t0 + n], reads=[B1["K0r"]], writes=[B1["K1r"]])
                    PCPY(KhT[1][96:128, t0:t0 + n], KhT[0][64:96, t0:t0 + n], reads=[B1["K0r"]], writes=[B1["K1r"]])

                watt = [wtile(base + 1, 3072), wtile(base + 2, 3072)]
                B2 = {k: Buf(k) for k in ["rt1", "rt2", "rcp0", "rcp1", "bT"]}
                PTB = [Buf("PT%d" % i) for i in range(4)]
                pj = {"i": 0}

                def pj_next():
                    b = 6 + pj["i"] % 2
                    pj["i"] += 1
                    return PS[b], PSB[b]

                def proj_units(h):
                    bf = h % 2
                    wt, wtB = watt[h // 8]
                    off = (h % 8) * 384
                    units = []

                    def qk_unit(t0, n):
                        psq, psqB = pj_next()
                        mm_group(psq[:, :n], [(wt[:, off + k * 128: off + (k + 1) * 128], cqnT[:, k, t0:t0 + n]) for k in range(2)],
                                 [wtB, B1["cqn"]], psqB)
                        CPY(QhT[bf][0:64, t0:t0 + n], psq[0:64, :n], reads=[psqB], writes=[QhB[bf]])
                        TT(QhT[bf][64:128, t0:t0 + n], psq[64:128, :n], rope[64:128, t0:t0 + n], ALU.mult,
                           reads=[psqB, cb["rope"]], writes=[QhB[bf]])
                        psk, pskB = pj_next()
                        mm_group(psk[:, :n], [(wt[:, off + 256: off + 384], ckvnT[:, t0:t0 + n])], [wtB, B1["ckvn"]], pskB)
                        CPY(KhT[bf][0:64, t0:t0 + n], psk[0:64, :n], reads=[pskB], writes=[KhB[bf]])

                    def v_unit(vg):
                        kts = KT[8 * vg: 8 * vg + 8]
                        psv, psvB = pj_next()
                        for j, (k0, kn) in enumerate(kts):
                            MM1(psv[:kn, j * 64:(j + 1) * 64], ckvnT[:, k0:k0 + kn], wt[:, off + 320: off + 384], True, True,
                                reads=[wtB, B1["ckvn"]] if j == 0 else (), writes=[psvB] if j == 0 else (), inc=(j == len(kts) - 1))
                        cnt = len(kts)
                        CPY(Vh[bf][:, 8 * vg: 8 * vg + cnt, 0:64], psv[:, :cnt * 64].rearrange("p (k d) -> p k d", k=cnt),
                            reads=[psvB], writes=[VhB[bf]])

                    for (t0, n) in T512:
                        units.append(lambda t0=t0, n=n: qk_unit(t0, n))
                    for vg in range(3):
                        units.append(lambda vg=vg: v_unit(vg))
                    return units

                u0 = proj_units(0)
                p1_front(0)
                p1_front(1)
                for bi in range(len(T512)):
                    zs = p1_z(bi)
                    p1_back(bi, zs)
                    if bi + 2 < len(T512):
                        p1_front(bi + 2)
                    u0[bi]()
                if stop_at is not None:
                    P.barrier()
                stop_check(s, l, 1)

                def attn(h, units):
                    bf = h % 2
                    tiles = []
                    for qb, (q0, qn) in enumerate(QB):
                        kts = [(ki, k0, kn) for ki, (k0, kn) in enumerate(KT) if k0 - q0 < qn]
                        for i, (ki, k0, kn) in enumerate(kts):
                            tiles.append((qb, q0, qn, i, len(kts), ki, k0, kn))
                    LA = 3
                    every = max(1, len(tiles) // (len(units) + 1)) if units else 0
                    for j in range(len(tiles) + LA):
                        if j < len(tiles):
                            (qb, q0, qn, i, nk, ki, k0, kn) = tiles[j]
                            d = k0 - q0
                            c0 = max(0, d)
                            ncol = qn - c0
                            pss, pssB = PS[j % 4], PSB[j % 4]
                            pt, ptB = PT[j % 4], PTB[j % 4]
                            mm_group(pss[:kn, :ncol], [(KhT[bf][:, k0:k0 + kn], QhT[bf][:, q0 + c0:q0 + qn])],
                                     [KhB[bf], QhB[bf], B1["K0r"], B1["K0r2"], B1["K1r"]], pssB)
                            ACT(pt[:kn, :ncol], pss[:kn, :ncol], AF.Exp, reads=[pssB], writes=[ptB], scale=SM_SCALE)
                            if d > -127:
                                w = min(qn, d + 128) - c0
                                sh = c0 - d
                                assert sh in (0, 16)
                                m = tri[:kn, 0:w] if sh == 0 else tri[:kn, 128:128 + w]
                                PTT(pt[:kn, :w], pt[:kn, :w], m, ALU.mult, reads=[ptB, cb["tri"]], writes=[ptB])
                        jj = j - LA
                        if jj >= 0:
                            (qb, q0, qn, i, nk, ki, k0, kn) = tiles[jj]
                            c0 = max(0, k0 - q0)
                            ncol = qn - c0
                            pso, psoB = PS[4 + qb % 2], PSB[4 + qb % 2]
                            pt, ptB = PT[jj % 4], PTB[jj % 4]
                            MM1(pso[:, c0:qn], Vh[bf][:kn, ki, :], pt[:kn, :ncol], i == 0, i == nk - 1,
                                reads=[ptB, VhB[bf]], writes=[psoB], inc=True)
                            if i == nk - 1:
                                r = qb % 2
                                rB = B2["rcp%d" % r]
                                RPOW(rcp[r][64:128, :qn], pso[64:128, :qn], -1.0, reads=[psoB], writes=[rB], wbuf=rB)
                                TT(bT[(h % 2) * 64:(h % 2) * 64 + 64, h // 2, q0:q0 + qn], pso[0:64, :qn], rcp[r][64:128, :qn], ALU.mult,
                                   reads=[psoB, rB], writes=[B2["bT"]])
                        if units and every and j % every == every - 1:
                            units.pop(0)()
                    while units:
                        units.pop(0)()

                for u in u0[len(T512):]:
                    u()
                for h in range(NH):
                    attn(h, proj_units(h + 1) if h + 1 < NH else [])
                P.barrier()
                stop_check(s, l, 2)

                psring["banks"] = list(range(8))
                WaB, WnB, haloB = Buf("Wa"), Buf("Wn"), Buf("halo3")
                for g in range(4):
                    TS(Wa[:, g, :], poolw[:, g, :], 1.0 / POOL_W[g], ALU.mult, reads=[cb["poolw"]], writes=[WaB])
                    TS(Wn[:, g, :], poolw[:, g, :], -1.0, ALU.mult, reads=[cb["poolw"]], writes=[WnB])
                BN = {k: Buf(k) for k in ["hn0", "hn1", "sq0", "sq1", "srt", "rstd"]}
                hTB3 = [[Buf("hTs%d_%d" % (i, b)) for b in range(NBLK)] for i in range(NSPAN)]
                blocks = [(b * BLK, BLK) for b in range(NBLK)]
                pre_normed = set()
                span_guard = []

                def hnB(bo):
                    return BN["hn%d" % (bo // BLK)]

                def norm_span(gcol, spi):
                    for (bo, n) in blocks:
                        rms_block([hT[:, c, spi * SPAN + bo:spi * SPAN + bo + n] for c in range(KC)], n,
                                  [sm(l, gcol + c) for c in range(KC)],
                                  [hn[:, c, bo:bo + n] for c in range(KC)], [hTB3[spi][bo // BLK]], [hnB(bo)],
                                  sq3, [BN["sq0"], BN["sq1"]], srt3, rstd3, BN["srt"], BN["rstd"])

                for sp in range(NSPAN):
                    T0 = sp * SPAN
                    B3 = {k: Buf(k) for k in ["U0", "U1", "A", "B", "y", "aT", "merged", "act",
                                              "sil0", "sil1", "sg0", "sg1", "sg2", "sg3", "mt0", "mt1", "mt2", "mt3", "ost", "t16"]}

                    def hTb(bo):
                        return hTB3[sp][bo // BLK]

                    if sp not in pre_normed:
                        norm_span(SM_GMIX, sp)
                    if span_guard:
                        P.wait_all("act", span_guard[0])
                        P.wait_all("dve", span_guard[0])
                        span_guard.clear()
                    wp_t, wpB = wtile(base + 3, KC * 512)
                    wp = wp_t[:, :].rearrange("p (k c) -> p k c", k=KC)
                    UB = [B3["U0"], B3["U1"], B3["A"], B3["B"]]
                    if sp == 0:
                        MSET(Ubf[:, :, 0:16], 0.0, UB)
                    else:
                        CPY(Ubf[:, :, 0:16], halo3[:, :, :], reads=[haloB], writes=UB)
                    for (bo, n) in blocks:
                        for g in range(4):
                            ps, psb = ps_next()
                            mm_group(ps[:, :n], [(wp[:, k, g * 128:(g + 1) * 128], hn[:, k, bo:bo + n]) for k in range(KC)],
                                     [wpB, hnB(bo)], psb)
                            ACT(Ubf[:, g, 16 + bo:16 + bo + n], ps[:, :n], AF.Copy, reads=[psb], writes=[UB[g]])
                    CPY(halo3[:, :, :], Ubf[:, :, W - 16:W], reads=UB, writes=[haloB])
                    for g in range(4):
                        wnd = POOL_W[g]
                        for (bo, n) in blocks:
                            ps, psb = ps_next()
                            pairs = [(Wa[:, g, :], Ubf[:, g, 16 + bo - sft:16 + bo - sft + n]) for sft in range(wnd)]
                            pairs.append((Wn[:, g, :], Ubf[:, g, 16 + bo:16 + bo + n]))
                            mm_group(ps[:, :n], pairs, [WaB, WnB, UB[g]], psb)
                            TS(aT[:, g, bo:bo + n], ps[:, :n], sm(l, SM_PSC + g), ALU.mult,
                               reads=[psb, cb["smalls"]], writes=[B3["aT"]])
                        if sp == 0:
                            p1, p1B = ps_next()
                            mm_group(p1[:, :16], [(Wa[:, g, :], Ubf[:, g, 16 - sft:32 - sft]) for sft in range(wnd)],
                                     [WaB, UB[g]], p1B)
                            p2, p2B = ps_next()
                            mm_group(p2[:, :16], [(Wn[:, g, :], Ubf[:, g, 16:32])], [WnB, UB[g]], p2B)
                            TT(tmp16[:, :], p1[:, :16], invc[:, g * 16:(g + 1) * 16], ALU.mult,
                               reads=[p1B, cb["invc"]], writes=[B3["t16"]])
                            TT(tmp16[:, :], tmp16[:, :], p2[:, :16], ALU.add, reads=[B3["t16"], p2B], writes=[B3["t16"]])
                            TS(aT[:, g, 0:16], tmp16[:, :], sm(l, SM_PSC + g), ALU.mult,
                               reads=[B3["t16"], cb["smalls"]], writes=[B3["aT"]])
                    gi = 0
                    for j in range(KC):
                        wc_t, wcB = wtile(base + 4 + j, 3584)
                        wga = wc_t[:, 0:1024].rearrange("p (k c) -> p k c", k=8)
                        wgb = wc_t[:, 1024:2048].rearrange("p (k c) -> p k c", k=8)
                        wpa = wc_t[:, 2048:2560].rearrange("p (k c) -> p k c", k=4)
                        wpb = wc_t[:, 2560:3584].rearrange("p (k c) -> p k c", k=8)
                        for (bo, n) in blocks:
                            pga, pgaB = ps_next()
                            mm_group(pga[:, :n], [(wga[:, k, :], hn[:, k, bo:bo + n]) for k in range(8)], [wcB, hnB(bo)], pgaB)
                            pgb, pgbB = ps_next()
                            mm_group(pgb[:, :n], [(wgb[:, k, :], hn[:, k, bo:bo + n]) for k in range(8)], [wcB, hnB(bo)], pgbB)
                            ppa, ppaB = ps_next()
                            mm_group(ppa[:, :n], [(wpa[:, k, :], aT[:, k, bo:bo + n]) for k in range(4)], [wcB, B3["aT"]], ppaB)
                            ppb, ppbB = ps_next()
                            mm_group(ppb[:, :n], [(wpb[:, k, :], bT[:, k, T0 + bo:T0 + bo + n]) for k in range(8)], [wcB, B2["bT"]], ppbB)
                            a0, a1 = (gi % 2) * 2, (gi % 2) * 2 + 1
                            gi += 1
                            ACT(sg[a0][:, :n], pga[:, :n], AF.Sigmoid, reads=[pgaB], writes=[B3["sg%d" % a0]])
                            ACT(sg[a1][:, :n], pgb[:, :n], AF.Sigmoid, reads=[pgbB], writes=[B3["sg%d" % a1]])
                            TT(mt[a0][:, :n], sg[a0][:, :n], ppa[:, :n], ALU.mult, reads=[ppaB, B3["sg%d" % a0]], writes=[B3["mt%d" % a0]])
                            TT(mt[a1][:, :n], sg[a1][:, :n], ppb[:, :n], ALU.mult, reads=[ppbB, B3["sg%d" % a1]], writes=[B3["mt%d" % a1]])
                            TT(merged[:, j, bo:bo + n], mt[a0][:, :n], mt[a1][:, :n], ALU.add,
                               reads=[B3["mt%d" % a0], B3["mt%d" % a1]], writes=[B3["merged"]])
                    wos = []
                    for half in range(2):
                        wo_t, woB = wtile(base + 12 + half, KC * 512)
                        wos.append((wo_t[:, :].rearrange("p (k c) -> p k c", k=KC), woB))

                    def wo_job(m, bo, n):
                        wo, woB = wos[m // 4]
                        mm = m % 4
                        ps, psb = ps_next()
                        mm_group(ps[:, :n], [(wo[:, k, mm * 128:(mm + 1) * 128], merged[:, k, bo:bo + n]) for k in range(KC)],
                                 [woB, B3["merged"]], psb)
                        TT(hT[:, m, T0 + bo:T0 + bo + n], ps[:, :n], hT[:, m, T0 + bo:T0 + bo + n], ALU.add,
                           reads=[psb, hTb(bo)], writes=[hTb(bo)])

                    def n2_args(bo, n):
                        return ([hT[:, c, T0 + bo:T0 + bo + n] for c in range(KC)], n,
                                [sm(l, SM_GFFN + c) for c in range(KC)], [hn[:, c, bo:bo + n] for c in range(KC)])

                    (b0, n0), (b1, n1) = blocks
                    for m in range(KC):
                        wo_job(m, b0, n0)
                    for m in range(4):
                        wo_job(m, b1, n1)
                    srcs0, _, g0, d0 = n2_args(b0, n0)
                    st0 = rms_part1(srcs0, n0, [hTb(b0)], sq3, [BN["sq0"], BN["sq1"]])
                    for m in range(4, KC):
                        wo_job(m, b1, n1)
                    guard = P.now()
                    rms_part2(st0, srcs0, n0, g0, d0, [hTb(b0)], [hnB(b0)], rstd3, BN["rstd"])
                    srcs1, _, g1, d1 = n2_args(b1, n1)
                    rms_block(srcs1, n1, g1, d1, [hTb(b1)], [hnB(b1)], sq3, [BN["sq0"], BN["sq1"]], srt3, rstd3, BN["srt"], BN["rstd"])
                    si = [0]

                    def ffn_job(wf, wfB, cc, c, bo, n):
                        pg, pgB = ps_next()
                        mm_group(pg[:, :n], [(wf[:, 2 * cc, k, :], hn[:, k, bo:bo + n]) for k in range(KC)], [wfB, hnB(bo)], pgB)
                        pu, puB = ps_next()
                        mm_group(pu[:, :n], [(wf[:, 2 * cc + 1, k, :], hn[:, k, bo:bo + n]) for k in range(KC)], [wfB, hnB(bo)], puB)
                        z = si[0] % 2
                        si[0] += 1
                        ACT(sil[z][:, :n], pg[:, :n], AF.Silu, reads=[pgB], writes=[B3["sil%d" % z]])
                        TT(actT[:, c, bo:bo + n], sil[z][:, :n], pu[:, :n], ALU.mult,
                           reads=[puB, B3["sil%d" % z]], writes=[B3["act"]], extra=guard)

                    wfs = []
                    for i in range(2):
                        wf_t, wfB = wtile(base + 14 + i, 4096)
                        wfs.append((wf_t[:, :].rearrange("p (a k c) -> p a k c", a=4, k=KC), wfB))
                    for (bo, n) in blocks:
                        for i in range(2):
                            for cc in range(2):
                                ffn_job(wfs[i][0], wfs[i][1], cc, 2 * i + cc, bo, n)
                    for i in range(2, FC // 2):
                        wf_t, wfB = wtile(base + 14 + i, 4096)
                        wf = wf_t[:, :].rearrange("p (a k c) -> p a k c", a=4, k=KC)
                        for cc in range(2):
                            for (bo, n) in blocks:
                                ffn_job(wf, wfB, cc, 2 * i + cc, bo, n)
                    for m in range(KC):
                        wd_t, wdB = wtile(base + 25 + m, FC * 128)
                        wd = wd_t[:, :FC * 128].rearrange("p (k c) -> p k c", k=FC)
                        for (bo, n) in blocks:
                            ps, psb = ps_next()
                            mm_group(ps[:, :n], [(wd[:, k, :], actT[:, k, bo:bo + n]) for k in range(FC)], [wdB, B3["act"]], psb)
                            TT(hT[:, m, T0 + bo:T0 + bo + n], ps[:, :n], hT[:, m, T0 + bo:T0 + bo + n], ALU.add,
                               reads=[psb, hTb(bo)], writes=[hTb(bo)])
                        if m == 4 and sp + 1 < NSPAN:
                            norm_span(SM_GMIX, sp + 1)
                            pre_normed.add(sp + 1)
                    if l == DEPTH - 1:
                        os_ = outT[s].rearrange("(c p) t -> p c t", p=128)
                        for (bo, n) in blocks:
                            t0 = T0 + bo
                            rms_block([hT[:, c, t0:t0 + n] for c in range(KC)], n, [sm(l, SM_GFIN + c) for c in range(KC)],
                                      [ost[:, c, :n] for c in range(KC)], [hTb(bo)], [ostB],
                                      sq3, [BN["sq0"], BN["sq1"]], srt3, rstd3, BN["srt"], BN["rstd"])
                            lo = max(t0, NM)
                            DMA("sp", os_[:, :, lo - NM:t0 + n - NM], ost[:, :, lo - t0:n], s_out, reads=[ostB])
                    if l == DEPTH - 1 and s + 1 < NS:
                        load_x(s + 1, [bi for bi, (t0, n) in enumerate(T512) if (t0 + n - 1) // SPAN == sp])
                    if sp + 1 < NSPAN:
                        span_guard.append(P.now())
                    else:
                        P.barrier(extra=[("d", s_out, P.dma_sems[s_out])] if P.dma_sems[s_out] else [])
                stop_check(s, l, 3)
        try:
            main_body()
        except _Stop:
            pass
        P.barrier()
        P.wait_all("sp", P.now() + ([("d", s_out, P.dma_sems[s_out])] if P.dma_sems[s_out] else []))
        P.emit()
    return nc


def _ktile(W, cols):
    K = W.shape[0]
    kc = K // 128
    return W[:, cols].reshape(kc, 128, len(cols)).transpose(1, 0, 2).reshape(128, kc * len(cols))


def _pack_weights(w_in, w_uq, w_ukv, w_pa, w_pb, w_o, w_gate, w_up, w_down):
    wst = np.zeros((DEPTH * NT_LAYER, 128, TW), np.float32)
    ar = np.arange
    for l in range(DEPTH):
        base = l * NT_LAYER
        kr = 896 + ar(32)
        kr_sw = 896 + (ar(32) + 16) % 32
        lat_cols = np.concatenate([512 + ar(256), 768 + ar(128), kr, kr_sw])
        wlat = np.zeros((D, 512), np.float32)
        wlat[:, :448] = w_in[l][:, lat_cols]
        wst[base + 0, :, :] = _ktile(wlat, ar(512))
        for t in range(2):
            for hh in range(8):
                h = t * 8 + hh
                qc = np.concatenate([h * 96 + ar(64), h * 96 + 64 + ar(32), h * 96 + 64 + (ar(32) + 16) % 32])
                o = hh * 384
                wst[base + 1 + t, :, o:o + 256] = _ktile(w_uq[l], qc)
                wst[base + 1 + t, :, o + 256:o + 320] = w_ukv[l][:, h * 128 + ar(64)]
                wst[base + 1 + t, :, o + 320:o + 384] = w_ukv[l][:, h * 128 + 64 + ar(64)]
        wst[base + 3, :, :] = _ktile(w_in[l], ar(512))
        for j in range(8):
            c = j * 128 + ar(128)
            wst[base + 4 + j, :, 0:1024] = _ktile(w_in[l], 928 + c)
            wst[base + 4 + j, :, 1024:2048] = _ktile(w_in[l], 1952 + c)
            wst[base + 4 + j, :, 2048:2560] = _ktile(w_pa[l], c)
            wst[base + 4 + j, :, 2560:3584] = _ktile(w_pb[l], c)
        for half in range(2):
            wst[base + 12 + half] = _ktile(w_o[l], half * 512 + ar(512))
        for i in range(FC // 2):
            for cc in range(2):
                c = (2 * i + cc) * 128 + ar(128)
                wst[base + 14 + i, :, (2 * cc) * 1024:(2 * cc + 1) * 1024] = _ktile(w_gate[l], c)
                wst[base + 14 + i, :, (2 * cc + 1) * 1024:(2 * cc + 2) * 1024] = _ktile(w_up[l], c)
        for m in range(8):
            wst[base + 25 + m, :, :FC * 128] = _ktile(w_down[l], m * 128 + ar(128))
    return wst


def _consts():
    inv = (1.0 / (np.float32(10000.0) ** (np.arange(0, 32, 2, dtype=np.float32) / np.float32(32)))).astype(np.float32)
    ang = (np.arange(L, dtype=np.float32)[:, None] * inv[None, :]).astype(np.float32)
    cos = np.cos(ang).astype(np.float32).T
    sin = np.sin(ang).astype(np.float32).T
    cos2 = np.concatenate([cos, cos], 0)
    sin2 = np.concatenate([-sin, sin], 0)
    rope = np.concatenate([cos2, sin2, cos2, sin2], 0).astype(np.float32)
    p = np.arange(128)
    tri = np.concatenate([(p[:, None] <= p[None, :]), (p[:, None] <= p[None, :] + 16)], 1).astype(np.float32)
    invc = np.zeros((128, 64), np.float32)
    for g, w in enumerate(POOL_W):
        invc[:, g * 16:(g + 1) * 16] = float(w) / np.minimum(np.arange(16) + 1.0, float(w))
    return rope, tri, invc


_CACHE = {}


def kernel(x, meta_tokens, norm_mix_g, w_in, pool_w, pool_scale, q_norm_g, kv_norm_g, w_uq, w_ukv,
           w_pa, w_pb, w_o, norm_ffn_g, w_gate, w_up, w_down, final_norm_g):
    f = lambda a: np.ascontiguousarray(np.asarray(a, dtype=np.float32))
    x = f(x)
    if "nc" not in _CACHE:
        _CACHE["nc"] = build_program()
    nc = _CACHE["nc"]
    wst = _pack_weights(f(w_in), f(w_uq), f(w_ukv), f(w_pa), f(w_pb), f(w_o), f(w_gate), f(w_up), f(w_down))
    smalls = np.zeros((128, DEPTH * SM_N), np.float32)
    col = lambda v: np.asarray(v, np.float32).reshape(-1, 128).T
    for l in range(DEPTH):
        o = l * SM_N
        smalls[:, o + SM_GMIX:o + SM_GMIX + 8] = col(norm_mix_g[l])
        smalls[:, o + SM_GFFN:o + SM_GFFN + 8] = col(norm_ffn_g[l])
        smalls[:, o + SM_PSC:o + SM_PSC + 4] = col(pool_scale[l])
        smalls[:, o + SM_QG:o + SM_QG + 2] = col(q_norm_g[l])
        smalls[:, o + SM_KVG:o + SM_KVG + 1] = col(kv_norm_g[l])
        smalls[:, o + SM_GFIN:o + SM_GFIN + 8] = col(final_norm_g)
    rope, tri, invc = _consts()
    metaT = np.ascontiguousarray(f(meta_tokens).T)
    poolw = f(pool_w)
    in_maps = []
    for c in range(NCORES):
        xs = np.ascontiguousarray(x[NS * c:NS * (c + 1)].transpose(0, 2, 1))
        in_maps.append({"xT": xs, "metaT": metaT, "wst": wst, "smalls": smalls, "poolw": poolw,
                        "rope": rope, "tri": tri, "invc": invc})
    res = run_bass_kernel_spmd(nc, in_maps, core_ids=list(range(NCORES)))
    out = np.empty((NCORES * NS, SEQ, D), np.float32)
    for c in range(NCORES):
        out[NS * c:NS * (c + 1)] = res.results[c]["outT"].transpose(0, 2, 1)
    return out
```

```python
import numpy as np
from contextlib import ExitStack
import concourse.bass as bass
import concourse.mybir as mybir
from concourse.bass_utils import run_bass_kernel_spmd

F32 = mybir.dt.float32
BF16 = mybir.dt.bfloat16
U8 = mybir.dt.uint8
AF = mybir.ActivationFunctionType
ALU = mybir.AluOpType

ENGS = ("pe", "act", "dve", "pool", "sp")

D = 1024
KC = 8
NM = 16
SEQ = 2048
L = NM + SEQ
LP = 17 * 128
NS = 2
NCORES = 8
DEPTH = 2
NH = 16
DFF = 2816
FC = DFF // 128
EPS = 1e-6
SM_SCALE = 96 ** -0.5
POOL_W = (2, 4, 8, 16)
NT_LAYER = 33
TW = 4096

T512 = [(0, 512), (512, 512), (1024, 512), (1536, 512), (2048, 16)]
KT = [(128 * i, 128) for i in range(17)]
QB = [(0, 16), (16, 512), (528, 512), (1040, 512), (1552, 512)]
BLK = 344
NSPAN = 3
SPAN = L // NSPAN
NBLK = SPAN // BLK

SM_GMIX, SM_GFFN, SM_PSC, SM_QG, SM_KVG, SM_GFIN, SM_N = 0, 8, 16, 20, 22, 23, 31


class Buf:
    __slots__ = ("name", "w", "r")

    def __init__(self, name):
        self.name = name
        self.w = []
        self.r = []


class Prog:
    def __init__(self, nc):
        self.nc = nc
        self.ops = {e: [] for e in ENGS}
        self.cnt = {e: 0 for e in ENGS}
        self.waited = {e: {} for e in ENGS}
        self.dma_sems = []

    def _need(self, eng, toks):
        w = self.waited[eng]
        best = {}
        for t in toks:
            if t is None:
                continue
            key = (t[0], t[1])
            if t[0] == "e" and t[1] == "pe" and eng == "pe":
                continue
            if w.get(key, 0) >= t[2]:
                continue
            if best.get(key, 0) < t[2]:
                best[key] = t[2]
        out = []
        for key, v in best.items():
            w[key] = v
            out.append((key[0], key[1], v))
        return out

    def _collect(self, reads, writes, extra):
        toks = list(extra)
        for b in reads:
            toks += b.w
        for b in writes:
            toks += b.w
            toks += b.r
        return toks

    def op(self, eng, fn, reads=(), writes=(), inc=True, extra=()):
        waits = self._need(eng, self._collect(reads, writes, extra))
        if inc:
            self.cnt[eng] += 1
            tok = ("e", eng, self.cnt[eng])
        else:
            tok = ("e", eng, self.cnt[eng] + 1)
        self.ops[eng].append((waits, fn, inc, None))
        for b in reads:
            b.r.append(tok)
        for b in writes:
            b.w = [t for t in b.w if (t[0], t[1]) != (tok[0], tok[1])] + [tok]
            b.r = []
        return tok

    def new_dma_sem(self):
        self.dma_sems.append(0)
        return len(self.dma_sems) - 1

    def dma(self, eng, fn, sem, reads=(), writes=(), extra=()):
        waits = self._need(eng, self._collect(reads, writes, extra))
        self.dma_sems[sem] += 16
        tok = ("d", sem, self.dma_sems[sem])
        self.ops[eng].append((waits, fn, False, sem))
        for b in reads:
            b.r.append(tok)
        for b in writes:
            b.w = [t for t in b.w if (t[0], t[1]) != (tok[0], tok[1])] + [tok]
            b.r = []
        return tok

    def wait_all(self, eng, toks):
        waits = self._need(eng, toks)
        if waits:
            self.ops[eng].append((waits, None, False, None))

    def now(self, engs=("pe", "act", "dve", "pool")):
        return [("e", e, self.cnt[e]) for e in engs if self.cnt[e] > 0]

    def barrier(self, engs=("pe", "act", "dve", "pool"), extra=()):
        toks = self.now(engs) + list(extra)
        for e in engs:
            self.wait_all(e, toks)

    def emit(self):
        nc = self.nc
        with ExitStack() as st:
            esem = {e: st.enter_context(nc.semaphore("s_" + e)) for e in ENGS}
            dsem = [st.enter_context(nc.semaphore("d_%d" % i)) for i in range(len(self.dma_sems))]
            block = st.enter_context(nc.Block())

            def run(e, engine):
                for waits, fn, inc, dsi in self.ops[e]:
                    for (k, a, v) in waits:
                        engine.wait_ge(esem[a] if k == "e" else dsem[a], v)
                    if fn is None:
                        continue
                    ins = fn(engine)
                    if dsi is not None:
                        ins.then_inc(dsem[dsi], 16)
                    elif inc:
                        ins.then_inc(esem[e], 1)

            @block.tensor
            def _(eng):
                run("pe", eng)

            @block.scalar
            def _(eng):
                run("act", eng)

            @block.vector
            def _(eng):
                run("dve", eng)

            @block.gpsimd
            def _(eng):
                run("pool", eng)

            @block.sync
            def _(eng):
                run("sp", eng)


class _Stop(Exception):
    pass


_DBG = {}


def build_program(stop_at=None):
    nc = bass.Bass("TRN2", target_bir_lowering=False)
    dram = lambda n, s, d, k: nc.dram_tensor(n, s, d, kind=k).ap()
    xT = dram("xT", [NS, D, SEQ], F32, "ExternalInput")
    metaT = dram("metaT", [D, NM], F32, "ExternalInput")
    wst = dram("wst", [DEPTH * NT_LAYER, 128, TW], F32, "ExternalInput")
    smalls_d = dram("smalls", [128, DEPTH * SM_N], F32, "ExternalInput")
    poolw_d = dram("poolw", [DEPTH, 4, 128, 128], F32, "ExternalInput")
    rope_d = dram("rope", [128, L], F32, "ExternalInput")
    tri_d = dram("tri", [128, 256], F32, "ExternalInput")
    invc_d = dram("invc", [128, 64], F32, "ExternalInput")
    ident_d = dram("ident", [128, 128], F32, "ExternalInput")
    outT = dram("outT", [NS, D, SEQ], F32, "ExternalOutput")

    with ExitStack() as st:
        sbt = lambda n, s, d: st.enter_context(nc.sbuf_tensor("sb_" + n, s, d))
        hT_t = sbt("hT", [128, KC * L], F32)
        hT = hT_t[:, :].rearrange("p (c t) -> p c t", c=KC)
        bT_t = sbt("bT", [128, KC * L], BF16)
        bT = bT_t[:, :].rearrange("p (c t) -> p c t", c=KC)
        rope = sbt("rope", [128, L], F32)
        WB = [sbt("wb%d" % i, [128, TW], BF16) for i in range(3)]
        ones = sbt("ones", [128, 128], BF16)
        tri = sbt("tri", [128, 256], BF16)
        ident = sbt("ident", [128, 128], BF16)
        smalls = sbt("smalls", [128, DEPTH * SM_N], F32)
        invc = sbt("invc", [128, 64], F32)
        poolw_t = sbt("poolw", [128, 4 * 128], BF16)
        poolw = poolw_t[:, :].rearrange("p (g d) -> p g d", g=4)
        ARENA_BYTES = int(nc.sbuf_bytes_remaining) - 2048
        arena = sbt("arena", [128, ARENA_BYTES], U8)
        PS = [st.enter_context(nc.psum_tensor("ps%d" % i, [128, 512], F32)) for i in range(8)]
        PSB = [Buf("ps%d" % i) for i in range(8)]

        P = Prog(nc)

        class Arena:
            def __init__(self):
                self.off = 0

            def alloc(self, shape_free, dt):
                esz = 4 if dt == F32 else 2
                n = int(np.prod(shape_free))
                nb = n * esz
                self.off = (self.off + 31) // 32 * 32
                o = self.off
                self.off += nb
                assert self.off <= ARENA_BYTES, ("arena overflow", self.off, ARENA_BYTES)
                v = arena[:, o:o + nb].bitcast(dt)
                _DBG.setdefault("allocs", []).append((o, nb, "f32" if dt == F32 else "bf16"))
                return v

        wsem = [P.new_dma_sem() for _ in range(3)]
        WBB = [Buf("wb%d" % i) for i in range(3)]
        s_const = P.new_dma_sem()
        s_h = [P.new_dma_sem() for _ in range(len(T512))]
        s_out = P.new_dma_sem()
        s_pw = P.new_dma_sem()
        wctr = [0]

        def MM1(out, lhsT, rhs, start, stop, reads=(), writes=(), inc=True):
            return P.op("pe", lambda e: e.matmul(out, lhsT, rhs, start=start, stop=stop), reads=reads, writes=writes, inc=inc)

        def mm_group(out, pairs, reads, psb):
            n = len(pairs)
            for i, (lhsT, rhs) in enumerate(pairs):
                MM1(out, lhsT, rhs, i == 0, i == n - 1, reads=reads if i == 0 else (), writes=[psb] if i == 0 else (),
                    inc=(i == n - 1))

        def ACT(out, in_, func, reads, writes, **kw):
            return P.op("act", lambda e: e.activation(out=out, in_=in_, func=func, **kw), reads=reads, writes=writes)

        def TT(out, in0, in1, op, reads, writes, extra=()):
            return P.op("dve", lambda e: e.tensor_tensor(out=out, in0=in0, in1=in1, op=op), reads=reads, writes=writes,
                        extra=extra)

        def STT(out, in0, scalar, in1, op0, op1, reads, writes):
            return P.op("dve", lambda e: e.scalar_tensor_tensor(out=out, in0=in0, scalar=scalar, in1=in1, op0=op0, op1=op1),
                        reads=reads, writes=writes)

        def TS(out, in0, scalar1, op0, reads, writes):
            return P.op("dve", lambda e: e.tensor_scalar(out=out, in0=in0, scalar1=scalar1, scalar2=None, op0=op0),
                        reads=reads, writes=writes)

        def RPOW(out, in_, pw, reads, writes, wbuf, scale=1.0, bias=0.0):
            ACT(out, in_, AF.Ln, reads=reads, writes=[wbuf], scale=scale, bias=bias)
            return ACT(out, out, AF.Exp, reads=[wbuf], writes=writes, scale=pw)

        def PCPY(out, in_, reads, writes):
            return P.op("pool", lambda e: e.tensor_copy(out=out, in_=in_), reads=reads, writes=writes)

        def PTT(out, in0, in1, op, reads, writes):
            return P.op("pool", lambda e: e.tensor_tensor(out=out, in0=in0, in1=in1, op=op), reads=reads, writes=writes)

        def CPY(out, in_, reads, writes):
            return P.op("dve", lambda e: e.tensor_copy(out=out, in_=in_), reads=reads, writes=writes)

        def MSET(ap, val, writes):
            return P.op("dve", lambda e: e.memset(ap, val), writes=writes)

        def DMA(eng, out, in_, sem, reads=(), writes=(), extra=()):
            return P.dma(eng, lambda e: e.dma_start(out=out, in_=in_), sem, reads=reads, writes=writes, extra=extra)

        def wtile(idx, used):
            i = wctr[0] % 3
            wctr[0] += 1
            DMA("pool", WB[i][:, :used], wst[idx, :, :used], wsem[i], writes=[WBB[i]])
            return WB[i], WBB[i]

        psring = {"i": 0, "banks": list(range(8))}

        def ps_next():
            b = psring["banks"][psring["i"] % len(psring["banks"])]
            psring["i"] += 1
            return PS[b], PSB[b]

        def sm(l, col):
            return smalls[:, l * SM_N + col: l * SM_N + col + 1]

        cb = {k: Buf(k) for k in ["rope", "smalls", "invc", "tri", "ones", "poolw", "halo"]}
        DMA("sp", rope[:], rope_d, s_const, writes=[cb["rope"]])
        DMA("sp", smalls[:], smalls_d, s_const, writes=[cb["smalls"]])
        DMA("sp", invc[:], invc_d, s_const, writes=[cb["invc"]])
        tok_const = ("d", s_const, P.dma_sems[s_const])
        for k in ["rope", "smalls", "invc"]:
            cb[k].w = [tok_const]
        s_tri = P.new_dma_sem()
        DMA("pool", tri[:], tri_d, s_tri, writes=[cb["tri"]])
        DMA("pool", ident[:], ident_d, s_tri, writes=[cb["tri"]])
        MSET(ones[:], 1.0, [cb["ones"]])

        A12 = Arena()
        cqnT = A12.alloc([2 * L], BF16).rearrange("p (c t) -> p c t", c=2)
        ckvnT = A12.alloc([LP], BF16)
        KhT = [A12.alloc([LP], BF16) for _ in range(2)]
        QhT = [A12.alloc([L], BF16) for _ in range(2)]
        Vh = [A12.alloc([17 * 128], BF16).rearrange("p (k d) -> p k d", k=17) for _ in range(2)]
        PT = [A12.alloc([512], BF16) for _ in range(4)]
        hnbs = [A12.alloc([KC * 512], BF16).rearrange("p (c t) -> p c t", c=KC) for _ in range(2)]
        sqq = A12.alloc([3 * 512], BF16).rearrange("p (c t) -> p c t", c=3)
        sqm = [A12.alloc([512], BF16) for _ in range(2)]
        srt = None
        rstd1 = A12.alloc([512], F32)
        srq = srt
        rq = A12.alloc([512], F32)
        srk = srt
        rk = A12.alloc([512], F32)
        rt1 = A12.alloc([512], F32)
        rcp = [A12.alloc([512], F32) for _ in range(2)]

        A3 = Arena()
        W = 16 + SPAN
        hn = A3.alloc([KC * SPAN], BF16).rearrange("p (c t) -> p c t", c=KC)
        actb = A3.alloc([FC * SPAN], BF16)
        act_off0 = A3.off - FC * SPAN * 2
        actT = actb.rearrange("p (c t) -> p c t", c=FC)
        srt3 = None
        rstd3 = A3.alloc([BLK], F32)
        sq3 = [A3.alloc([BLK], BF16) for _ in range(2)]
        sil = [A3.alloc([BLK], F32) for _ in range(2)]
        sg = [A3.alloc([BLK], F32) for _ in range(4)]
        mt = [A3.alloc([BLK], F32) for _ in range(4)]
        ost = A3.alloc([KC * BLK], F32).rearrange("p (c t) -> p c t", c=KC)
        tmp16 = A3.alloc([16], F32)
        Wa = A3.alloc([4 * 128], BF16).rearrange("p (g d) -> p g d", g=4)
        Wn = A3.alloc([4 * 128], BF16).rearrange("p (g d) -> p g d", g=4)
        halo3 = A3.alloc([4 * 16], BF16).rearrange("p (g t) -> p g t", g=4)
        A3b = Arena()
        A3b.off = act_off0
        merged = A3b.alloc([KC * SPAN], BF16).rearrange("p (c t) -> p c t", c=KC)
        aT = A3b.alloc([4 * SPAN], BF16).rearrange("p (c t) -> p c t", c=4)
        Ubf = A3b.alloc([4 * W], BF16).rearrange("p (g t) -> p g t", g=4)
        assert A3b.off <= act_off0 + FC * SPAN * 2, "alias region overflow"

        hTB = [Buf("hT%d" % i) for i in range(len(T512))]
        ostB = Buf("ost")

        def rms_block(srcs, n, gcols, dsts, rd, wr, sq_bufs, sqB, srt_ap, rstd_ap, srtB, rstdB, inv_n=1.0 / D):
            st8 = rms_part1(srcs, n, rd, sq_bufs, sqB)
            rms_part2(st8, srcs, n, gcols, dsts, rd, wr, rstd_ap, rstdB, inv_n)

        def rms_part1(srcs, n, rd, sq_bufs, sqB):
            nch = len(srcs)
            psA, psAB = ps_next()
            for c in range(nch):
                j = c % 2
                ACT(sq_bufs[j][:, :n], srcs[c], AF.Square, reads=rd, writes=[sqB[j]])
                MM1(psA[:, :n], ones[:, :], sq_bufs[j][:, :n], c == 0, c == nch - 1,
                    reads=[sqB[j], cb["ones"]], writes=[psAB], inc=True)
            return psA, psAB

        def rms_part2(st8, srcs, n, gcols, dsts, rd, wr, rstd_ap, rstdB, inv_n=1.0 / D):
            psA, psAB = st8
            RPOW(rstd_ap[:, :n], psA[:, :n], -0.5, reads=[psAB], writes=[rstdB], wbuf=rstdB, scale=inv_n, bias=EPS)
            for c in range(len(srcs)):
                STT(dsts[c], srcs[c], gcols[c], rstd_ap[:, :n], ALU.mult, ALU.mult,
                    reads=list(rd) + [rstdB, cb["smalls"]], writes=wr)

        def stop_check(s, l, ph):
            if stop_at == (s, l, ph):
                raise _Stop()

        preloaded = set()

        def load_x(s, bis):
            prev = P.now()
            xs = xT[s].rearrange("(c p) t -> p c t", p=128)
            ms = metaT.rearrange("(c p) t -> p c t", p=128)
            for bi in bis:
                t0, n = T512[bi]
                lo = max(t0, NM)
                DMA("sp", hT[:, :, lo:t0 + n], xs[:, :, lo - NM:t0 + n - NM], s_h[bi], writes=[hTB[bi]], extra=prev)
                if t0 == 0:
                    DMA("sp", hT[:, :, 0:NM], ms, s_h[bi], writes=[hTB[bi]], extra=prev)
                preloaded.add((s, bi))

        def main_body():
          for s in range(NS):
            load_x(s, [bi for bi in range(len(T512)) if (s, bi) not in preloaded])

            for l in range(DEPTH):
                base = l * NT_LAYER
                DMA("pool", poolw, poolw_d[l].rearrange("g c d -> c g d"), s_pw, writes=[cb["poolw"]])

                psring["banks"] = list(range(8))
                wlat_t, wlatB = wtile(base + 0, KC * 512)
                wlat = wlat_t[:, :].rearrange("p (k c) -> p k c", k=KC)
                B1 = {k: Buf(k) for k in ["hnb0", "hnb1", "sqq", "sqm0", "sqm1", "srt", "rstd", "rq", "rk", "rt1", "rt2",
                                          "cqn", "ckvn", "K0r", "K0r2", "K1r"]}
                B1["srq"] = B1["srt"]
                B1["srk"] = B1["srt"]
                KhB = [Buf("Kh0"), Buf("Kh1")]
                QhB = [Buf("Qh0"), Buf("Qh1")]
                VhB = [Buf("Vh0"), Buf("Vh1")]
                for i in range(2):
                    MSET(Vh[i][:, :, 64:128], 1.0, [VhB[i]])
                    MSET(Vh[i][:, 16, 64:128], 0.0, [VhB[i]])
                    MSET(Vh[i][0:16, 16, 64:128], 1.0, [VhB[i]])
                    MSET(KhT[i][:, L:LP], 0.0, [B1["K%dr" % i]])
                MSET(ckvnT[:, L:LP], 0.0, [B1["ckvn"]])
                def p1_front(bi):
                    t0, n = T512[bi]
                    hnb, hnbB = hnbs[bi % 2], B1["hnb%d" % (bi % 2)]
                    rms_block([hT[:, c, t0:t0 + n] for c in range(KC)], n, [sm(l, SM_GMIX + c) for c in range(KC)],
                              [hnb[:, c, :n] for c in range(KC)], [hTB[bi]], [hnbB],
                              sqm, [B1["sqm0"], B1["sqm1"]], srt, rstd1, B1["srt"], B1["rstd"])

                def p1_z(bi):
                    t0, n = T512[bi]
                    hnb, hnbB = hnbs[bi % 2], B1["hnb%d" % (bi % 2)]
                    zs = []
                    for (c0, M) in [(0, 128), (128, 128), (256, 128), (384, 128)]:
                        ps, psb = ps_next()
                        mm_group(ps[:M, :n], [(wlat[:, k, c0:c0 + M], hnb[:, k, :n]) for k in range(KC)],
                                 [wlatB, hnbB], psb)
                        zs.append((ps, psb))
                    return zs

                def p1_back(bi, zs):
                    t0, n = T512[bi]
                    (pq0, pq0B), (pq1, pq1B), (pkv, pkvB), (pkr, pkrB) = zs
                    for j, (pz, pzB) in enumerate([(pq0, pq0B), (pq1, pq1B), (pkv, pkvB)]):
                        ACT(sqq[:, j, :n], pz[:, :n], AF.Square, reads=[pzB], writes=[B1["sqq"]])
                    psQ, psQB = ps_next()
                    mm_group(psQ[:, :n], [(ones[:, :], sqq[:, 0, :n]), (ones[:, :], sqq[:, 1, :n])], [B1["sqq"], cb["ones"]], psQB)
                    psK, psKB = ps_next()
                    mm_group(psK[:, :n], [(ones[:, :], sqq[:, 2, :n])], [B1["sqq"], cb["ones"]], psKB)
                    RPOW(rq[:, :n], psQ[:, :n], -0.5, reads=[psQB], writes=[B1["rq"]], wbuf=B1["rq"], scale=1.0 / 256, bias=EPS)
                    RPOW(rk[:, :n], psK[:, :n], -0.5, reads=[psKB], writes=[B1["rk"]], wbuf=B1["rk"], scale=1.0 / 128, bias=EPS)
                    for j, (pz, pzB) in enumerate([(pq0, pq0B), (pq1, pq1B)]):
                        STT(cqnT[:, j, t0:t0 + n], pz[:, :n], sm(l, SM_QG + j), rq[:, :n], ALU.mult, ALU.mult,
                            reads=[pzB, B1["rq"], cb["smalls"]], writes=[B1["cqn"]])
                    STT(ckvnT[:, t0:t0 + n], pkv[:, :n], sm(l, SM_KVG), rk[:, :n], ALU.mult, ALU.mult,
                        reads=[pkvB, B1["rk"], cb["smalls"]], writes=[B1["ckvn"]])
                    TT(rt1[0:32, :n], pkr[0:32, :n], rope[0:32, t0:t0 + n], ALU.mult, reads=[pkrB, cb["rope"]], writes=[B1["rt1"]])
                    TT(pkr[32:64, :n], pkr[32:64, :n], rope[32:64, t0:t0 + n], ALU.mult, reads=[pkrB, cb["rope"]], writes=[pkrB])
                    TT(KhT[0][64:96, t0:t0 + n], pkr[32:64, :n], rt1[0:32, :n], ALU.add,
                       reads=[B1["rt1"], pkrB], writes=[B1["K0r"]])
                    PCPY(KhT[0][96:128, t0:t0 + n], KhT[0][64:96, t0:t0 + n], reads=[B1["K0r"]], writes=[B1["K0r2"]])
                    PCPY(KhT[1][64:96, t0:t0 + n], KhT[0][64:96, t0:t0 + n], reads=[B1["K0r"]], writes=[B1["K1r"]])
                    PCPY(KhT[1][96:128, t0:t0 + n], KhT[0][64:96, t0:t0 + n], reads=[B1["K0r"]], writes=[B1["K1r"]])

                watt = [wtile(base + 1, 3072), wtile(base + 2, 3072)]
                B2 = {k: Buf(k) for k in ["rt1", "rt2", "rcp0", "rcp1", "bT"]}
                PTB = [Buf("PT%d" % i) for i in range(4)]
                pj = {"i": 0}

                def pj_next():
                    b = 6 + pj["i"] % 2
                    pj["i"] += 1
                    return PS[b], PSB[b]

                def proj_units(h):
                    bf = h % 2
                    wt, wtB = watt[h // 8]
                    off = (h % 8) * 384
                    units = []

                    def qk_unit(t0, n):
                        psq, psqB = pj_next()
                        mm_group(psq[:, :n], [(wt[:, off + k * 128: off + (k + 1) * 128], cqnT[:, k, t0:t0 + n]) for k in range(2)],
                                 [wtB, B1["cqn"]], psqB)
                        CPY(QhT[bf][0:64, t0:t0 + n], psq[0:64, :n], reads=[psqB], writes=[QhB[bf]])
                        TT(QhT[bf][64:128, t0:t0 + n], psq[64:128, :n], rope[64:128, t0:t0 + n], ALU.mult,
                           reads=[psqB, cb["rope"]], writes=[QhB[bf]])
                        psk, pskB = pj_next()
                        mm_group(psk[:, :n], [(wt[:, off + 256: off + 384], ckvnT[:, t0:t0 + n])], [wtB, B1["ckvn"]], pskB)
                        CPY(KhT[bf][0:64, t0:t0 + n], psk[0:64, :n], reads=[pskB], writes=[KhB[bf]])

                    def v_unit(vg):
                        kts = KT[8 * vg: 8 * vg + 8]
                        psv, psvB = pj_next()
                        for j, (k0, kn) in enumerate(kts):
                            MM1(psv[:kn, j * 64:(j + 1) * 64], ckvnT[:, k0:k0 + kn], wt[:, off + 320: off + 384], True, True,
                                reads=[wtB, B1["ckvn"]] if j == 0 else (), writes=[psvB] if j == 0 else (), inc=(j == len(kts) - 1))
                        cnt = len(kts)
                        CPY(Vh[bf][:, 8 * vg: 8 * vg + cnt, 0:64], psv[:, :cnt * 64].rearrange("p (k d) -> p k d", k=cnt),
                            reads=[psvB], writes=[VhB[bf]])

                    for (t0, n) in T512:
                        units.append(lambda t0=t0, n=n: qk_unit(t0, n))
                    for vg in range(3):
                        units.append(lambda vg=vg: v_unit(vg))
                    return units

                u0 = proj_units(0)
                p1_front(0)
                p1_front(1)
                for bi in range(len(T512)):
                    zs = p1_z(bi)
                    p1_back(bi, zs)
                    if bi + 2 < len(T512):
                        p1_front(bi + 2)
                    if bi >= 1:
                        u0[bi - 1]()
                if stop_at is not None:
                    P.barrier()
                stop_check(s, l, 1)

                def attn(h, units):
                    bf = h % 2
                    tiles = []
                    for qb, (q0, qn) in enumerate(QB):
                        kts = [(ki, k0, kn) for ki, (k0, kn) in enumerate(KT) if k0 - q0 < qn]
                        for i, (ki, k0, kn) in enumerate(kts):
                            tiles.append((qb, q0, qn, i, len(kts), ki, k0, kn))
                    LA = 3
                    deferred = []
                    every = max(1, len(tiles) // (len(units) + 1)) if units else 0
                    for j in range(len(tiles) + LA):
                        if j < len(tiles):
                            (qb, q0, qn, i, nk, ki, k0, kn) = tiles[j]
                            d = k0 - q0
                            c0 = max(0, d)
                            ncol = qn - c0
                            pss, pssB = PS[j % 4], PSB[j % 4]
                            pt, ptB = PT[j % 4], PTB[j % 4]
                            partial = d > -127
                            MM1(pss[:kn, :ncol], KhT[bf][:, k0:k0 + kn], QhT[bf][:, q0 + c0:q0 + qn], True, not partial,
                                reads=[KhB[bf], QhB[bf], B1["K0r"], B1["K0r2"], B1["K1r"]], writes=[pssB], inc=not partial)
                            if partial:
                                w = min(qn, d + 128) - c0
                                sh = c0 - d
                                assert sh in (0, 16)
                                m = tri[:, 0:w] if sh == 0 else tri[:, 128:128 + w]
                                MM1(pss[:kn, :w], ident[:, :], m, False, True, reads=[cb["tri"]], writes=[pssB], inc=True)
                            ACT(pt[:kn, :ncol], pss[:kn, :ncol], AF.Exp, reads=[pssB], writes=[ptB], scale=SM_SCALE)
                        jj = j - LA
                        if jj >= 0:
                            (qb, q0, qn, i, nk, ki, k0, kn) = tiles[jj]
                            c0 = max(0, k0 - q0)
                            ncol = qn - c0
                            pso, psoB = PS[4 + qb % 2], PSB[4 + qb % 2]
                            pt, ptB = PT[jj % 4], PTB[jj % 4]
                            MM1(pso[:, c0:qn], Vh[bf][:kn, ki, :], pt[:kn, :ncol], i == 0, i == nk - 1,
                                reads=[ptB, VhB[bf]], writes=[psoB], inc=True)
                            if i == nk - 1:
                                def finish(pso=pso, psoB=psoB, qb=qb, q0=q0, qn=qn):
                                    r = qb % 2
                                    rB = B2["rcp%d" % r]
                                    RPOW(rcp[r][64:128, :qn], pso[64:128, :qn], -1.0, reads=[psoB], writes=[rB], wbuf=rB)
                                    TT(bT[(h % 2) * 64:(h % 2) * 64 + 64, h // 2, q0:q0 + qn], pso[0:64, :qn], rcp[r][64:128, :qn],
                                       ALU.mult, reads=[psoB, rB], writes=[B2["bT"]])
                                deferred.append([4, finish])
                        for dfr in deferred:
                            dfr[0] -= 1
                        while deferred and deferred[0][0] <= 0:
                            deferred.pop(0)[1]()
                        if units and every and j % every == every - 1:
                            units.pop(0)()
                    while deferred:
                        deferred.pop(0)[1]()
                    while units:
                        units.pop(0)()

                for u in u0[len(T512) - 1:]:
                    u()
                for h in range(NH):
                    attn(h, proj_units(h + 1) if h + 1 < NH else [])
                P.barrier()
                stop_check(s, l, 2)

                psring["banks"] = list(range(8))
                WaB, WnB, haloB = Buf("Wa"), Buf("Wn"), Buf("halo3")
                for g in range(4):
                    TS(Wa[:, g, :], poolw[:, g, :], 1.0 / POOL_W[g], ALU.mult, reads=[cb["poolw"]], writes=[WaB])
                    TS(Wn[:, g, :], poolw[:, g, :], -1.0, ALU.mult, reads=[cb["poolw"]], writes=[WnB])
                BN = {k: Buf(k) for k in ["hn0", "hn1", "sq0", "sq1", "srt", "rstd"]}
                hTB3 = [[Buf("hTs%d_%d" % (i, b)) for b in range(NBLK)] for i in range(NSPAN)]
                blocks = [(b * BLK, BLK) for b in range(NBLK)]
                pre_normed = set()
                span_guard = []

                def hnB(bo):
                    return BN["hn%d" % (bo // BLK)]

                def norm_span(gcol, spi):
                    for (bo, n) in blocks:
                        rms_block([hT[:, c, spi * SPAN + bo:spi * SPAN + bo + n] for c in range(KC)], n,
                                  [sm(l, gcol + c) for c in range(KC)],
                                  [hn[:, c, bo:bo + n] for c in range(KC)], [hTB3[spi][bo // BLK]], [hnB(bo)],
                                  sq3, [BN["sq0"], BN["sq1"]], srt3, rstd3, BN["srt"], BN["rstd"])

                for sp in range(NSPAN):
                    T0 = sp * SPAN
                    B3 = {k: Buf(k) for k in ["U0", "U1", "A", "B", "y", "aT", "merged", "act",
                                              "sil0", "sil1", "sg0", "sg1", "sg2", "sg3", "mt0", "mt1", "mt2", "mt3", "ost", "t16"]}

                    def hTb(bo):
                        return hTB3[sp][bo // BLK]

                    if sp not in pre_normed:
                        norm_span(SM_GMIX, sp)
                    if span_guard:
                        P.wait_all("act", span_guard[0])
                        P.wait_all("dve", span_guard[0])
                        span_guard.clear()
                    wp_t, wpB = wtile(base + 3, KC * 512)
                    wp = wp_t[:, :].rearrange("p (k c) -> p k c", k=KC)
                    UB = [B3["U0"], B3["U1"], B3["A"], B3["B"]]
                    if sp == 0:
                        MSET(Ubf[:, :, 0:16], 0.0, UB)
                    else:
                        CPY(Ubf[:, :, 0:16], halo3[:, :, :], reads=[haloB], writes=UB)
                    for (bo, n) in blocks:
                        for g in range(4):
                            ps, psb = ps_next()
                            mm_group(ps[:, :n], [(wp[:, k, g * 128:(g + 1) * 128], hn[:, k, bo:bo + n]) for k in range(KC)],
                                     [wpB, hnB(bo)], psb)
                            ACT(Ubf[:, g, 16 + bo:16 + bo + n], ps[:, :n], AF.Copy, reads=[psb], writes=[UB[g]])
                    CPY(halo3[:, :, :], Ubf[:, :, W - 16:W], reads=UB, writes=[haloB])
                    for g in range(4):
                        wnd = POOL_W[g]
                        for (bo, n) in blocks:
                            ps, psb = ps_next()
                            pairs = [(Wa[:, g, :], Ubf[:, g, 16 + bo - sft:16 + bo - sft + n]) for sft in range(wnd)]
                            pairs.append((Wn[:, g, :], Ubf[:, g, 16 + bo:16 + bo + n]))
                            mm_group(ps[:, :n], pairs, [WaB, WnB, UB[g]], psb)
                            TS(aT[:, g, bo:bo + n], ps[:, :n], sm(l, SM_PSC + g), ALU.mult,
                               reads=[psb, cb["smalls"]], writes=[B3["aT"]])
                        if sp == 0:
                            p1, p1B = ps_next()
                            mm_group(p1[:, :16], [(Wa[:, g, :], Ubf[:, g, 16 - sft:32 - sft]) for sft in range(wnd)],
                                     [WaB, UB[g]], p1B)
                            p2, p2B = ps_next()
                            mm_group(p2[:, :16], [(Wn[:, g, :], Ubf[:, g, 16:32])], [WnB, UB[g]], p2B)
                            TT(tmp16[:, :], p1[:, :16], invc[:, g * 16:(g + 1) * 16], ALU.mult,
                               reads=[p1B, cb["invc"]], writes=[B3["t16"]])
                            TT(tmp16[:, :], tmp16[:, :], p2[:, :16], ALU.add, reads=[B3["t16"], p2B], writes=[B3["t16"]])
                            TS(aT[:, g, 0:16], tmp16[:, :], sm(l, SM_PSC + g), ALU.mult,
                               reads=[B3["t16"], cb["smalls"]], writes=[B3["aT"]])
                    gi = 0
                    for j in range(KC):
                        wc_t, wcB = wtile(base + 4 + j, 3584)
                        wga = wc_t[:, 0:1024].rearrange("p (k c) -> p k c", k=8)
                        wgb = wc_t[:, 1024:2048].rearrange("p (k c) -> p k c", k=8)
                        wpa = wc_t[:, 2048:2560].rearrange("p (k c) -> p k c", k=4)
                        wpb = wc_t[:, 2560:3584].rearrange("p (k c) -> p k c", k=8)
                        for (bo, n) in blocks:
                            pga, pgaB = ps_next()
                            mm_group(pga[:, :n], [(wga[:, k, :], hn[:, k, bo:bo + n]) for k in range(8)], [wcB, hnB(bo)], pgaB)
                            pgb, pgbB = ps_next()
                            mm_group(pgb[:, :n], [(wgb[:, k, :], hn[:, k, bo:bo + n]) for k in range(8)], [wcB, hnB(bo)], pgbB)
                            ppa, ppaB = ps_next()
                            mm_group(ppa[:, :n], [(wpa[:, k, :], aT[:, k, bo:bo + n]) for k in range(4)], [wcB, B3["aT"]], ppaB)
                            ppb, ppbB = ps_next()
                            mm_group(ppb[:, :n], [(wpb[:, k, :], bT[:, k, T0 + bo:T0 + bo + n]) for k in range(8)], [wcB, B2["bT"]], ppbB)
                            a0, a1 = (gi % 2) * 2, (gi % 2) * 2 + 1
                            gi += 1
                            ACT(sg[a0][:, :n], pga[:, :n], AF.Sigmoid, reads=[pgaB], writes=[B3["sg%d" % a0]])
                            ACT(sg[a1][:, :n], pgb[:, :n], AF.Sigmoid, reads=[pgbB], writes=[B3["sg%d" % a1]])
                            TT(mt[a0][:, :n], sg[a0][:, :n], ppa[:, :n], ALU.mult, reads=[ppaB, B3["sg%d" % a0]], writes=[B3["mt%d" % a0]])
                            TT(mt[a1][:, :n], sg[a1][:, :n], ppb[:, :n], ALU.mult, reads=[ppbB, B3["sg%d" % a1]], writes=[B3["mt%d" % a1]])
                            TT(merged[:, j, bo:bo + n], mt[a0][:, :n], mt[a1][:, :n], ALU.add,
                               reads=[B3["mt%d" % a0], B3["mt%d" % a1]], writes=[B3["merged"]])
                    wos = []
                    for half in range(2):
                        wo_t, woB = wtile(base + 12 + half, KC * 512)
                        wos.append((wo_t[:, :].rearrange("p (k c) -> p k c", k=KC), woB))

                    def wo_job(m, bo, n):
                        wo, woB = wos[m // 4]
                        mm = m % 4
                        ps, psb = ps_next()
                        mm_group(ps[:, :n], [(wo[:, k, mm * 128:(mm + 1) * 128], merged[:, k, bo:bo + n]) for k in range(KC)],
                                 [woB, B3["merged"]], psb)
                        TT(hT[:, m, T0 + bo:T0 + bo + n], ps[:, :n], hT[:, m, T0 + bo:T0 + bo + n], ALU.add,
                           reads=[psb, hTb(bo)], writes=[hTb(bo)])

                    def n2_args(bo, n):
                        return ([hT[:, c, T0 + bo:T0 + bo + n] for c in range(KC)], n,
                                [sm(l, SM_GFFN + c) for c in range(KC)], [hn[:, c, bo:bo + n] for c in range(KC)])

                    (b0, n0), (b1, n1) = blocks
                    for m in range(KC):
                        wo_job(m, b0, n0)
                    for m in range(4):
                        wo_job(m, b1, n1)
                    srcs0, _, g0, d0 = n2_args(b0, n0)
                    st0 = rms_part1(srcs0, n0, [hTb(b0)], sq3, [BN["sq0"], BN["sq1"]])
                    for m in range(4, KC):
                        wo_job(m, b1, n1)
                    guard = P.now()
                    rms_part2(st0, srcs0, n0, g0, d0, [hTb(b0)], [hnB(b0)], rstd3, BN["rstd"])
                    srcs1, _, g1, d1 = n2_args(b1, n1)
                    rms_block(srcs1, n1, g1, d1, [hTb(b1)], [hnB(b1)], sq3, [BN["sq0"], BN["sq1"]], srt3, rstd3, BN["srt"], BN["rstd"])
                    si = [0]

                    def ffn_job(wf, wfB, cc, c, bo, n):
                        pg, pgB = ps_next()
                        mm_group(pg[:, :n], [(wf[:, 2 * cc, k, :], hn[:, k, bo:bo + n]) for k in range(KC)], [wfB, hnB(bo)], pgB)
                        pu, puB = ps_next()
                        mm_group(pu[:, :n], [(wf[:, 2 * cc + 1, k, :], hn[:, k, bo:bo + n]) for k in range(KC)], [wfB, hnB(bo)], puB)
                        z = si[0] % 2
                        si[0] += 1
                        ACT(sil[z][:, :n], pg[:, :n], AF.Silu, reads=[pgB], writes=[B3["sil%d" % z]])
                        TT(actT[:, c, bo:bo + n], sil[z][:, :n], pu[:, :n], ALU.mult,
                           reads=[puB, B3["sil%d" % z]], writes=[B3["act"]], extra=guard)

                    wfs = []
                    for i in range(2):
                        wf_t, wfB = wtile(base + 14 + i, 4096)
                        wfs.append((wf_t[:, :].rearrange("p (a k c) -> p a k c", a=4, k=KC), wfB))
                    for (bo, n) in blocks:
                        for i in range(2):
                            for cc in range(2):
                                ffn_job(wfs[i][0], wfs[i][1], cc, 2 * i + cc, bo, n)
                    for i in range(2, FC // 2):
                        wf_t, wfB = wtile(base + 14 + i, 4096)
                        wf = wf_t[:, :].rearrange("p (a k c) -> p a k c", a=4, k=KC)
                        for cc in range(2):
                            for (bo, n) in blocks:
                                ffn_job(wf, wfB, cc, 2 * i + cc, bo, n)
                    for m in range(KC):
                        wd_t, wdB = wtile(base + 25 + m, FC * 128)
                        wd = wd_t[:, :FC * 128].rearrange("p (k c) -> p k c", k=FC)
                        for (bo, n) in blocks:
                            ps, psb = ps_next()
                            mm_group(ps[:, :n], [(wd[:, k, :], actT[:, k, bo:bo + n]) for k in range(FC)], [wdB, B3["act"]], psb)
                            TT(hT[:, m, T0 + bo:T0 + bo + n], ps[:, :n], hT[:, m, T0 + bo:T0 + bo + n], ALU.add,
                               reads=[psb, hTb(bo)], writes=[hTb(bo)])
                        if m == 4 and sp + 1 < NSPAN:
                            norm_span(SM_GMIX, sp + 1)
                            pre_normed.add(sp + 1)
                    if l == DEPTH - 1:
                        os_ = outT[s].rearrange("(c p) t -> p c t", p=128)
                        for (bo, n) in blocks:
                            t0 = T0 + bo
                            rms_block([hT[:, c, t0:t0 + n] for c in range(KC)], n, [sm(l, SM_GFIN + c) for c in range(KC)],
                                      [ost[:, c, :n] for c in range(KC)], [hTb(bo)], [ostB],
                                      sq3, [BN["sq0"], BN["sq1"]], srt3, rstd3, BN["srt"], BN["rstd"])
                            lo = max(t0, NM)
                            DMA("sp", os_[:, :, lo - NM:t0 + n - NM], ost[:, :, lo - t0:n], s_out, reads=[ostB])
                    if l == DEPTH - 1 and s + 1 < NS:
                        load_x(s + 1, [bi for bi, (t0, n) in enumerate(T512) if (t0 + n - 1) // SPAN == sp])
                    if sp + 1 < NSPAN:
                        span_guard.append(P.now())
                    else:
                        P.barrier(extra=[("d", s_out, P.dma_sems[s_out])] if P.dma_sems[s_out] else [])
                stop_check(s, l, 3)
        try:
            main_body()
        except _Stop:
            pass
        P.barrier()
        P.wait_all("sp", P.now() + ([("d", s_out, P.dma_sems[s_out])] if P.dma_sems[s_out] else []))
        P.emit()
    return nc


def _ktile(W, cols):
    K = W.shape[0]
    kc = K // 128
    return W[:, cols].reshape(kc, 128, len(cols)).transpose(1, 0, 2).reshape(128, kc * len(cols))


def _pack_weights(w_in, w_uq, w_ukv, w_pa, w_pb, w_o, w_gate, w_up, w_down):
    wst = np.zeros((DEPTH * NT_LAYER, 128, TW), np.float32)
    ar = np.arange
    for l in range(DEPTH):
        base = l * NT_LAYER
        kr = 896 + ar(32)
        kr_sw = 896 + (ar(32) + 16) % 32
        lat_cols = np.concatenate([512 + ar(256), 768 + ar(128), kr, kr_sw])
        wlat = np.zeros((D, 512), np.float32)
        wlat[:, :448] = w_in[l][:, lat_cols]
        wst[base + 0, :, :] = _ktile(wlat, ar(512))
        for t in range(2):
            for hh in range(8):
                h = t * 8 + hh
                qc = np.concatenate([h * 96 + ar(64), h * 96 + 64 + ar(32), h * 96 + 64 + (ar(32) + 16) % 32])
                o = hh * 384
                wst[base + 1 + t, :, o:o + 256] = _ktile(w_uq[l], qc)
                wst[base + 1 + t, :, o + 256:o + 320] = w_ukv[l][:, h * 128 + ar(64)]
                wst[base + 1 + t, :, o + 320:o + 384] = w_ukv[l][:, h * 128 + 64 + ar(64)]
        wst[base + 3, :, :] = _ktile(w_in[l], ar(512))
        for j in range(8):
            c = j * 128 + ar(128)
            wst[base + 4 + j, :, 0:1024] = _ktile(w_in[l], 928 + c)
            wst[base + 4 + j, :, 1024:2048] = _ktile(w_in[l], 1952 + c)
            wst[base + 4 + j, :, 2048:2560] = _ktile(w_pa[l], c)
            wst[base + 4 + j, :, 2560:3584] = _ktile(w_pb[l], c)
        for half in range(2):
            wst[base + 12 + half] = _ktile(w_o[l], half * 512 + ar(512))
        for i in range(FC // 2):
            for cc in range(2):
                c = (2 * i + cc) * 128 + ar(128)
                wst[base + 14 + i, :, (2 * cc) * 1024:(2 * cc + 1) * 1024] = _ktile(w_gate[l], c)
                wst[base + 14 + i, :, (2 * cc + 1) * 1024:(2 * cc + 2) * 1024] = _ktile(w_up[l], c)
        for m in range(8):
            wst[base + 25 + m, :, :FC * 128] = _ktile(w_down[l], m * 128 + ar(128))
    return wst


def _consts():
    inv = (1.0 / (np.float32(10000.0) ** (np.arange(0, 32, 2, dtype=np.float32) / np.float32(32)))).astype(np.float32)
    ang = (np.arange(L, dtype=np.float32)[:, None] * inv[None, :]).astype(np.float32)
    cos = np.cos(ang).astype(np.float32).T
    sin = np.sin(ang).astype(np.float32).T
    cos2 = np.concatenate([cos, cos], 0)
    sin2 = np.concatenate([-sin, sin], 0)
    rope = np.concatenate([cos2, sin2, cos2, sin2], 0).astype(np.float32)
    p = np.arange(128)
    vis = np.concatenate([(p[:, None] <= p[None, :]), (p[:, None] <= p[None, :] + 16)], 1)
    tri = np.where(vis, 0.0, -30000.0).astype(np.float32)
    invc = np.zeros((128, 64), np.float32)
    for g, w in enumerate(POOL_W):
        invc[:, g * 16:(g + 1) * 16] = float(w) / np.minimum(np.arange(16) + 1.0, float(w))
    return rope, tri, invc


_CACHE = {}


def kernel(x, meta_tokens, norm_mix_g, w_in, pool_w, pool_scale, q_norm_g, kv_norm_g, w_uq, w_ukv,
           w_pa, w_pb, w_o, norm_ffn_g, w_gate, w_up, w_down, final_norm_g):
    f = lambda a: np.ascontiguousarray(np.asarray(a, dtype=np.float32))
    x = f(x)
    if "nc" not in _CACHE:
        _CACHE["nc"] = build_program()
    nc = _CACHE["nc"]
    wst = _pack_weights(f(w_in), f(w_uq), f(w_ukv), f(w_pa), f(w_pb), f(w_o), f(w_gate), f(w_up), f(w_down))
    smalls = np.zeros((128, DEPTH * SM_N), np.float32)
    col = lambda v: np.asarray(v, np.float32).reshape(-1, 128).T
    for l in range(DEPTH):
        o = l * SM_N
        smalls[:, o + SM_GMIX:o + SM_GMIX + 8] = col(norm_mix_g[l])
        smalls[:, o + SM_GFFN:o + SM_GFFN + 8] = col(norm_ffn_g[l])
        smalls[:, o + SM_PSC:o + SM_PSC + 4] = col(pool_scale[l])
        smalls[:, o + SM_QG:o + SM_QG + 2] = col(q_norm_g[l])
        smalls[:, o + SM_KVG:o + SM_KVG + 1] = col(kv_norm_g[l])
        smalls[:, o + SM_GFIN:o + SM_GFIN + 8] = col(final_norm_g)
    rope, tri, invc = _consts()
    metaT = np.ascontiguousarray(f(meta_tokens).T)
    poolw = f(pool_w)
    in_maps = []
    for c in range(NCORES):
        xs = np.ascontiguousarray(x[NS * c:NS * (c + 1)].transpose(0, 2, 1))
        in_maps.append({"xT": xs, "metaT": metaT, "wst": wst, "smalls": smalls, "poolw": poolw,
                        "rope": rope, "tri": tri, "invc": invc, "ident": np.eye(128, dtype=np.float32)})
    res = run_bass_kernel_spmd(nc, in_maps, core_ids=list(range(NCORES)))
    out = np.empty((NCORES * NS, SEQ, D), np.float32)
    for c in range(NCORES):
        out[NS * c:NS * (c + 1)] = res.results[c]["outT"].transpose(0, 2, 1)
    return out
```
